# Optimizing a Trainium2 kernel written in Bass

```python
import jax, jax.numpy as jnp
from jax import lax
import numpy as np

D_MODEL = 1024
BATCH = 4
SEQ = 8192
DEPTH = 4
DEC_BATCH = 8
DEC_SEQ = 32
PAST_LEN = 1024

CHUNK = 64
MIX_WIDTH = D_MODEL
SB_WIDTH = MIX_WIDTH // 2
SB_HEAD_DIM = 64
N_SB_HEADS = SB_WIDTH // SB_HEAD_DIM
POOL_WIDTH = MIX_WIDTH - SB_WIDTH
POOL_WINDOWS = (2, 4, 8, 16)
N_POOL_GROUPS = len(POOL_WINDOWS)
POOL_GROUP_DIM = POOL_WIDTH // N_POOL_GROUPS
POOL_HIST = 15
D_FF = ((8 * D_MODEL // 3 + 255) // 256) * 256
PLE_DIM = 256
Q_BLOCK = 128
EPS = 1e-6

kernel_name = 'stickbreak_pool_hybrid_stream_step'


def _rmsnorm(x, g):
    xf = x.astype(jnp.float32)
    y = xf * lax.rsqrt(jnp.mean(xf * xf, axis=-1, keepdims=True) + EPS)
    return (y * g.astype(jnp.float32)).astype(x.dtype)


def _stick_breaking(q, k, v, q_pos, k_pos):
    b, h, sq, dh = q.shape
    blk = min(Q_BLOCK, sq)
    nb = sq // blk
    scale = dh ** -0.5
    kf = k.astype(jnp.float32)
    vf = v.astype(jnp.float32)
    qb = jnp.moveaxis(q.reshape(b, h, nb, blk, dh), 2, 0)
    pb = q_pos.reshape(nb, blk)

    def one_block(args):
        qi, pi = args
        z = jnp.einsum('bhqd,bhkd->bhqk', qi.astype(jnp.float32), kf) * scale
        mask = k_pos[None, :] < pi[:, None]
        log_keep = jnp.where(mask, jax.nn.log_sigmoid(-z), 0.0)
        suffix = lax.cumsum(log_keep, axis=3, reverse=True) - log_keep
        a = jnp.where(mask, jnp.exp(jax.nn.log_sigmoid(z) + suffix), 0.0)
        return jnp.einsum('bhqk,bhkd->bhqd', a, vf)

    out = lax.map(one_block, (qb, pb))
    return jnp.moveaxis(out, 0, 2).reshape(b, h, sq, dh).astype(v.dtype)


def _multiscale_pool(u, hist, pos, w_pool, pool_scale):
    b, s, c = u.shape
    full = jnp.concatenate([hist.astype(u.dtype), u], axis=1)
    fullf = full.astype(jnp.float32)
    cs = jnp.concatenate([jnp.zeros((b, 1, c), jnp.float32), jnp.cumsum(fullf, axis=1)], axis=1)
    end = POOL_HIST + 1
    uf = u.astype(jnp.float32)
    diffs = []
    for g, w in enumerate(POOL_WINDOWS):
        lo, hi = g * POOL_GROUP_DIM, (g + 1) * POOL_GROUP_DIM
        win_sum = cs[:, end:end + s, lo:hi] - cs[:, end - w:end - w + s, lo:hi]
        cnt = jnp.minimum(w, pos + 1).astype(jnp.float32)[None, :, None]
        diffs.append(win_sum / cnt - uf[:, :, lo:hi])
    d = jnp.stack(diffs, axis=2)
    y = jnp.einsum('bsgc,gcd->bsgd', d, w_pool.astype(jnp.float32)).reshape(b, s, c)
    y = _rmsnorm(y, pool_scale).astype(u.dtype)
    return y, full[:, -POOL_HIST:]


def _layer(x, p, past_k, past_v, pool_hist, q_pos, k_pos,
           g_mix, w_in, g_sb_out, w_pool, pool_scale, w_out,
           g_ffn, w_ffn_gate, w_ffn_up, w_ffn_down, g_ple, w_ple, w_ple_gate):
    b, s, _ = x.shape
    h = _rmsnorm(x, g_mix)
    u = h @ w_in
    q, k, v, up = jnp.split(u, [SB_WIDTH, 2 * SB_WIDTH, 3 * SB_WIDTH], axis=-1)
    q = q.reshape(b, s, N_SB_HEADS, SB_HEAD_DIM).transpose(0, 2, 1, 3)
    k = k.reshape(b, s, N_SB_HEADS, SB_HEAD_DIM).transpose(0, 2, 1, 3)
    v = v.reshape(b, s, N_SB_HEADS, SB_HEAD_DIM).transpose(0, 2, 1, 3)
    if past_k is None:
        k_all, v_all = k, v
    else:
        k_all = jnp.concatenate([past_k.astype(k.dtype), k], axis=2)
        v_all = jnp.concatenate([past_v.astype(v.dtype), v], axis=2)
    att = _stick_breaking(q, k_all, v_all, q_pos, k_pos)
    att = _rmsnorm(att.transpose(0, 2, 1, 3).reshape(b, s, SB_WIDTH), g_sb_out)
    pool, new_hist = _multiscale_pool(up, pool_hist, q_pos, w_pool, pool_scale)
    x = x + jnp.concatenate([att, pool], axis=-1) @ w_out
    h2 = _rmsnorm(x, g_ffn)
    x = x + (jax.nn.silu(h2 @ w_ffn_gate) * (h2 @ w_ffn_up)) @ w_ffn_down
    gate = jax.nn.sigmoid(_rmsnorm(x, g_ple) @ w_ple_gate)
    x = x + (p @ w_ple) * gate
    return x, k, v, new_hist


def setup_inputs(seed: int = 0) -> dict:
    key = jax.random.key(seed)
    ks = jax.random.split(key, 24)
    f32 = jnp.float32

    def nrm(k, shape, scale):
        return jax.random.normal(k, shape, f32) * scale

    def gain(k, shape):
        return 1.0 + 0.02 * jax.random.normal(k, shape, f32)

    return {
        'x_prompt': nrm(ks[0], (BATCH, SEQ, D_MODEL), 1.0),
        'x_sample': nrm(ks[1], (DEC_BATCH, DEC_SEQ, D_MODEL), 1.0),
        'cache_k': nrm(ks[2], (DEPTH, DEC_BATCH, N_SB_HEADS, PAST_LEN, SB_HEAD_DIM), 1.0),
        'cache_v': nrm(ks[3], (DEPTH, DEC_BATCH, N_SB_HEADS, PAST_LEN, SB_HEAD_DIM), 1.0),
        'state_pool': nrm(ks[4], (DEPTH, DEC_BATCH, POOL_HIST, POOL_WIDTH), 1.0),
        'p_prompt': nrm(ks[5], (DEPTH, BATCH, SEQ, PLE_DIM), 1.0),
        'p_sample': nrm(ks[6], (DEPTH, DEC_BATCH, DEC_SEQ, PLE_DIM), 1.0),
        'g_mix': gain(ks[7], (DEPTH, D_MODEL)),
        'w_in': nrm(ks[8], (DEPTH, D_MODEL, 3 * SB_WIDTH + POOL_WIDTH), D_MODEL ** -0.5),
        'g_sb_out': gain(ks[9], (DEPTH, SB_WIDTH)),
        'w_pool': nrm(ks[10], (DEPTH, N_POOL_GROUPS, POOL_GROUP_DIM, POOL_GROUP_DIM), POOL_GROUP_DIM ** -0.5),
        'pool_scale': gain(ks[11], (DEPTH, POOL_WIDTH)),
        'w_out': nrm(ks[12], (DEPTH, MIX_WIDTH, D_MODEL), MIX_WIDTH ** -0.5),
        'g_ffn': gain(ks[13], (DEPTH, D_MODEL)),
        'w_ffn_gate': nrm(ks[14], (DEPTH, D_MODEL, D_FF), D_MODEL ** -0.5),
        'w_ffn_up': nrm(ks[15], (DEPTH, D_MODEL, D_FF), D_MODEL ** -0.5),
        'w_ffn_down': nrm(ks[16], (DEPTH, D_FF, D_MODEL), D_FF ** -0.5),
        'g_ple': gain(ks[17], (DEPTH, D_MODEL)),
        'w_ple': nrm(ks[18], (DEPTH, PLE_DIM, D_MODEL), PLE_DIM ** -0.5),
        'w_ple_gate': nrm(ks[19], (DEPTH, D_MODEL, D_MODEL), D_MODEL ** -0.5),
        'g_final': gain(ks[20], (D_MODEL,)),
    }


def reference(x_prompt, x_sample, cache_k, cache_v, state_pool, p_prompt, p_sample,
              g_mix, w_in, g_sb_out, w_pool, pool_scale, w_out,
              g_ffn, w_ffn_gate, w_ffn_up, w_ffn_down, g_ple, w_ple, w_ple_gate, g_final):
    past = cache_k.shape[3]
    seq = x_prompt.shape[1]
    dseq = x_sample.shape[1]
    q_pos_p = jnp.arange(seq, dtype=jnp.int32)
    k_pos_p = q_pos_p
    q_pos_s = past + jnp.arange(dseq, dtype=jnp.int32)
    k_pos_s = jnp.arange(past + dseq, dtype=jnp.int32)
    hist_p = jnp.zeros((x_prompt.shape[0], POOL_HIST, POOL_WIDTH), x_prompt.dtype)

    xp, xs = x_prompt, x_sample
    kp_l, vp_l, sp_l, ks_l, vs_l, ss_l = [], [], [], [], [], []
    for l in range(DEPTH):
        lw = (g_mix[l], w_in[l], g_sb_out[l], w_pool[l], pool_scale[l], w_out[l],
              g_ffn[l], w_ffn_gate[l], w_ffn_up[l], w_ffn_down[l], g_ple[l], w_ple[l], w_ple_gate[l])
        xp, kp, vp, sp = _layer(xp, p_prompt[l], None, None, hist_p, q_pos_p, k_pos_p, *lw)
        xs, kk, vv, ss = _layer(xs, p_sample[l], cache_k[l], cache_v[l], state_pool[l],
                                q_pos_s, k_pos_s, *lw)
        kp_l.append(kp); vp_l.append(vp); sp_l.append(sp)
        ks_l.append(kk); vs_l.append(vv); ss_l.append(ss)

    y_prompt = _rmsnorm(xp, g_final)
    y_sample = _rmsnorm(xs, g_final)
    return (y_prompt, y_sample,
            jnp.stack(kp_l), jnp.stack(vp_l), jnp.stack(sp_l),
            jnp.stack(ks_l), jnp.stack(vs_l), jnp.stack(ss_l))
```

```python
import numpy as np
import concourse.bass as bass
import concourse.mybir as mybir
from concourse.bass_utils import run_bass_kernel_spmd

F32 = mybir.dt.float32
BF16 = mybir.dt.bfloat16
AF = mybir.ActivationFunctionType
ALU = mybir.AluOpType

D = 1024
DEPTH = 4
S = 8192
NBATCH = 4
H = 8
DH = 64
PW = 512
DFF = 2816
PLE = 256
CH = 512
DS = 32
PAST = 1024
EPS = 1e-6
NSLOT = 28
NRING = 4
SEGB = 16
EPOCH = 12000
NEPOCH = 24
WINDOWS = (2, 4, 8, 16)


class Buf:
    __slots__ = ("w", "r", "const")

    def __init__(self, const=False):
        self.w = None
        self.r = {}
        self.const = const


class Sched:
    ENGS = ("pe", "act", "dve", "pool", "sp")

    def __init__(self):
        self.ops = {e: [] for e in self.ENGS}
        self.cnt = {e: 0 for e in self.ENGS}
        self.known = {e: {} for e in self.ENGS}
        self.dsem = {}

    def emit(self, eng, fn, reads=(), writes=(), extra=(), dma_sem=None):
        deps = {}

        def add(ev):
            if ev is None:
                return
            k, v = ev
            if deps.get(k, 0) < v:
                deps[k] = v

        for b in reads:
            add(b.w)
        for b in writes:
            add(b.w)
            for k, v in b.r.items():
                add((k, v))
        for ev in extra:
            add(ev)
        waits = []
        kn = self.known[eng]
        for k, v in deps.items():
            if eng == "pe" and k[0] == "pe":
                continue
            if kn.get(k, 0) >= v:
                continue
            kn[k] = v
            waits.append((k, v))
        if dma_sem is not None:
            self.dsem[dma_sem] = self.dsem.get(dma_sem, 0) + 16
            ev = ((dma_sem, 0), self.dsem[dma_sem])
            inc = ((dma_sem, 0), 16)
        else:
            c = self.cnt[eng]
            self.cnt[eng] = c + 1
            ev = ((eng, c // EPOCH), c % EPOCH + 1)
            inc = (ev[0], 1)
        self.ops[eng].append((waits, fn, inc))
        for b in reads:
            if not b.const:
                if b.r.get(ev[0], 0) < ev[1]:
                    b.r[ev[0]] = ev[1]
        for b in writes:
            b.w = ev
            b.r = {}
        return ev


class _Stop(Exception):
    pass


def build(NCH=16, DO_SAMPLE=True, STOP=None, NOCONV=False, segb=SEGB):
    nc = bass.Bass("TRN2", target_bir_lowering=False)

    stage_cnt = {}

    def stage(name):
        stage_cnt[name] = stage_cnt.get(name, 0) + 1
        if STOP is not None:
            sn, _, sk = STOP.partition(":")
            if sn == name and stage_cnt[name] == int(sk or 1):
                raise _Stop()
    sc = Sched()

    def din(name, shape, dt=F32):
        return nc.dram_tensor(name, list(shape), dt, kind="ExternalInput").ap()

    def dout(name, shape, dt=F32):
        return nc.dram_tensor(name, list(shape), dt, kind="ExternalOutput").ap()

    def dint(name, shape, dt):
        return nc.dram_tensor(name, list(shape), dt, kind="Internal").ap()

    xp = din("xp", [S, D])
    pp = din("pp", [DEPTH, S, PLE])
    xs = din("xs", [DS, D])
    ps_ = din("ps", [DEPTH, DS, PLE])
    ck = din("ck", [DEPTH, H, PAST, DH])
    cv = din("cv", [DEPTH, H, PAST, DH])
    stp = din("stp", [DEPTH, 15, PW])
    w_in = din("w_in", [DEPTH, D, 2048])
    w_pool = din("w_pool", [DEPTH, 4, 128, 128])
    w_out = din("w_out", [DEPTH, D, D])
    w_g = din("w_g", [DEPTH, D, DFF])
    w_u = din("w_u", [DEPTH, D, DFF])
    w_d = din("w_d", [DEPTH, DFF, D])
    w_ple = din("w_ple", [DEPTH, PLE, D])
    w_pg = din("w_pg", [DEPTH, D, D])
    gcols_d = din("gcols", [128, 136])
    masks_d = din("masks", [128, 4, 512])
    invc_d = din("invc", [128, 4, 16])
    cmat_d = din("cmat", [128, 4, 128])

    yp = dout("yp", [S, D])
    ys = dout("ys", [DS, D])
    nkp = dout("nkp", [DEPTH, H, S, DH])
    nvp = dout("nvp", [DEPTH, H, S, DH])
    npp = dout("npp", [DEPTH, 15, PW])
    nks = dout("nks", [DEPTH, H, DS, DH])
    nvs = dout("nvs", [DEPTH, H, DS, DH])
    nps = dout("nps", [DEPTH, 15, PW])

    wscr = dint("wscr", [DEPTH, NSLOT, 128, 4096], BF16)
    kts = dint("kts", [DEPTH, 4, 128, S], BF16)
    vsc = dint("vsc", [DEPTH, 4, 128, S // 128, 128], BF16)

    def sb(name, shape, dt):
        return nc.alloc_sbuf_tensor("sb_" + name, list(shape), dt)

    xT = sb("xT", [128, 8, CH], F32)
    hT = sb("hT", [128, 8, CH], BF16)
    sq = sb("sq", [128, 2, CH], BF16)
    rstd = sb("rstd", [128, CH], F32)
    lnt = sb("lnt", [128, CH], F32)
    qT = sb("qT", [128, 4, CH], BF16)
    kT = sb("kT", [128, 4, CH], BF16)
    vtok = sb("vtok", [128, 4, 8, 65], BF16)
    stg = sb("stg", [128, 2, 512], F32)
    upx = sb("upx", [128, 4, 16 + CH], F32)
    pt = sb("pt", [128, 3, 16 + CH], F32)
    dT = sb("dT", [128, 2, CH], BF16)
    praw = sb("praw", [128, 4, CH], F32)
    attT = sb("attT", [128, 4, CH], F32)
    mixT = sb("mixT", [128, 8, CH], BF16)
    e32 = sb("e32", [128, 2, CH], F32)
    spb = sb("spb", [128, 3, CH], BF16)
    acc = sb("acc", [128, 4, 128], F32)
    tmpP = sb("tmpP", [128, 2, 4, 64], F32)
    Rst = sb("Rst", [128, 2, 4], F32)
    mmt = sb("mmt", [128, 2, 4], F32)
    ab = sb("ab", [128, 3, CH], BF16)
    ktseg = sb("ktseg", [128, 2, SEGB * 128], BF16)
    vseg = sb("vseg", [128, 2, SEGB, 2, 65], BF16)
    actT = sb("actT", [128, 22, CH], BF16)
    ring = sb("ring", [128, NRING, 4096], BF16)
    xstg = sb("xstg", [128, 1024], F32)
    pstg = sb("pstg", [128, 4, 256], F32)
    pT = sb("pT", [128, 2, CH], BF16)
    masks = sb("masks", [128, 4, 512], BF16)
    wpool = sb("wpool", [128, 16, 128], BF16)
    gcols = sb("gcols", [128, 136], F32)
    invc = sb("invc", [128, 4, 16], F32)
    cmat32 = sb("cmat32", [128, 4, 128], F32)
    cmatb = sb("cmatb", [128, 4, 128], BF16)
    khalo = sb("khalo", [128, DEPTH, 4, 16], F32)
    sgt = sb("sgt", [128, 2, CH], F32)
    tmp32 = sb("tmp32", [128, 2, CH], F32)
    psum = nc.alloc_psum_tensor("psum", [128, 8, 512], F32)

    B = {}

    def buf(name, const=False):
        if name not in B:
            B[name] = Buf(const)
        return B[name]

    ident32 = cmat32[:, 0, :]
    negtri = cmatb[:, 1, :]
    negones = cmatb[:, 2, :]
    ones = cmatb[:, 3, :]
    CONST = buf("const", True)

    class BankRR:
        def __init__(self, ids):
            self.ids = ids
            self.i = 0

        def get(self):
            b = self.ids[self.i % len(self.ids)]
            self.i += 1
            return b

    zrr = BankRR([2, 3, 4, 5, 6, 7])

    def bank(b):
        return psum[:, b, :]

    def bbuf(b):
        return buf("bank%d" % b)

    emit = sc.emit

    ring_state = {"n": 0}
    ring_bufs = [buf("ring%d" % i) for i in range(NRING)]
    wconv_bufs = [buf("wconv%d" % l) for l in range(DEPTH)]

    converted = set()

    def slot_pieces(l, s):
        r = lambda ap: ap.rearrange("(k p) n -> p k n", p=128)
        if s < 4:
            return [(0, 8, 512, r(w_in[l, :, 512 * s:512 * s + 512]))]
        if s < 6:
            return [(0, 8, 512, r(w_out[l, :, 512 * (s - 4):512 * (s - 4) + 512]))]
        if s < 17:
            i = s - 6
            return [(0, 8, 256, r(w_g[l, :, 256 * i:256 * i + 256])), (2048, 8, 256, r(w_u[l, :, 256 * i:256 * i + 256]))]
        if s < 25:
            j = s - 17
            return [(0, 22, 128, r(w_d[l, :, 128 * j:128 * j + 128]))]
        if s < 27:
            return [(0, 8, 512, r(w_pg[l, :, 512 * (s - 25):512 * (s - 25) + 512]))]
        return [(0, 2, 1024, r(w_ple[l, :, :]))]

    SLOT_ORDER = list(range(0, 27))
    SLOT_NELEM = {s_: (2816 if 17 <= s_ < 25 else (2048 if s_ == 27 else 4096)) for s_ in range(NSLOT)}
    ngroups = NCH + (1 if DO_SAMPLE else 0)
    slot_plan = [(l_, s_) for _g in range(ngroups) for l_ in range(DEPTH) for s_ in SLOT_ORDER]
    plan_state = {"issued": 0, "used": 0}
    PF = NRING - 1

    def issue_slot(n):
        l, s = slot_plan[n]
        nelem = SLOT_NELEM[s]
        i = n % NRING
        rb = ring_bufs[i]
        wsb = buf("wscr_%d_%d" % (l, s))
        if (l, s) not in converted:
            converted.add((l, s))
            for (off, kt, ncol, src) in slot_pieces(l, s):
                emit("pool", lambda e, i=i, off=off, kt=kt, ncol=ncol, src=src: e.dma_start(
                    out=ring[:, i, off:off + kt * ncol].rearrange("p (k n) -> p k n", k=kt), in_=src),
                    writes=[rb], dma_sem="ring%d" % i)
            emit("sp", lambda e, l=l, s=s, i=i, nelem=nelem: e.dma_start(out=wscr[l, s, :, 0:nelem], in_=ring[:, i, 0:nelem]),
                 reads=[rb], writes=[wsb], dma_sem="wst")
        else:
            emit("sp", lambda e, l=l, s=s, i=i, nelem=nelem: e.dma_start(out=ring[:, i, 0:nelem], in_=wscr[l, s, :, 0:nelem]),
                 reads=[wsb], writes=[rb], dma_sem="ring%d" % i)

    wple_t = sb("wple_t", [128, 2048], BF16)

    def load_wple(l):
        wb_ = buf("wple_t")
        wsb = buf("wscr_%d_%d" % (l, 27))
        if (l, 27) not in converted:
            converted.add((l, 27))
            for (off, kt, ncol, src) in slot_pieces(l, 27):
                emit("pool", lambda e, src=src: e.dma_start(out=wple_t[:, :].rearrange("p (k n) -> p k n", k=2), in_=src),
                     writes=[wb_], dma_sem="wple")
            emit("sp", lambda e: e.dma_start(out=wscr[l, 27, :, 0:2048], in_=wple_t[:, :]),
                 reads=[wb_], writes=[wsb], dma_sem="wst")
        else:
            emit("sp", lambda e: e.dma_start(out=wple_t[:, :], in_=wscr[l, 27, :, 0:2048]),
                 reads=[wsb], writes=[wb_], dma_sem="wple")
        return wb_

    def load_slot(l, s, nelem=4096):
        n = plan_state["used"]
        assert slot_plan[n] == (l, s), (slot_plan[n], l, s)
        plan_state["used"] += 1
        while plan_state["issued"] < min(len(slot_plan), n + PF + 1):
            issue_slot(plan_state["issued"])
            plan_state["issued"] += 1
        return n % NRING, ring_bufs[n % NRING]

    def prologue():
        emit("sp", lambda e: e.dma_start(out=gcols[:], in_=gcols_d[:, :]), writes=[buf("gcols")], dma_sem="c0")
        emit("sp", lambda e: e.dma_start(out=invc[:], in_=invc_d[:, :, :]), writes=[buf("invc")], dma_sem="c1")
        emit("sp", lambda e: e.dma_start(out=cmat32[:], in_=cmat_d[:, :, :]), writes=[buf("cmat32")], dma_sem="c2")
        emit("pool", lambda e: e.dma_start(out=cmatb[:], in_=cmat_d[:, :, :]), writes=[buf("cmatb")], dma_sem="c3")
        emit("pool", lambda e: e.dma_start(out=masks[:], in_=masks_d[:, :, :]), writes=[buf("masks")], dma_sem="c4")
        for l in range(DEPTH):
            emit("pool", lambda e, l=l: e.dma_start(out=wpool[:, 4 * l:4 * l + 4, :],
                                                    in_=w_pool[l].rearrange("g c d -> c g d")),
                 writes=[buf("wpool")], dma_sem="c5")
        emit("dve", lambda e: e.memset(khalo[:], 0.0), writes=[buf("khalo%d" % l) for l in range(DEPTH)])
        emit("dve", lambda e: e.memset(vtok[:, :, :, 64:65], 1.0), writes=[buf("vtok%d" % t_) for t_ in range(4)])
        emit("dve", lambda e: e.memset(vseg[:, :, :, :, 64:65].rearrange("p a b c d -> p (a b c) d"), 1.0), writes=[buf("vseg0"), buf("vseg1")])

    def gcol(l, which, i):
        if l == DEPTH:
            c = 128 + i
        else:
            c = 32 * l + (0, 8, 16, 24, 28)[which] + i
        return gcols[:, c:c + 1]

    def rmsnorm(N, srcs, gaps, outs, width):
        nt = len(srcs)
        b = zrr.get()
        for i, (sap, sbuf_) in enumerate(srcs):
            j = i % 2
            emit("act", lambda e, sap=sap, j=j: e.activation(out=sq[:, j, :N], in_=sap, func=AF.Square),
                 reads=[sbuf_], writes=[buf("sq%d" % j)])
            emit("pe", lambda e, b=b, j=j, i=i: e.matmul(bank(b)[:, :N], lhsT=ones, rhs=sq[:, j, :N],
                                                         start=(i == 0), stop=(i == nt - 1)),
                 reads=[buf("sq%d" % j), buf("cmatb")], writes=[bbuf(b)])
        emit("act", lambda e, b=b: e.activation(out=lnt[:, :N], in_=bank(b)[:, :N], func=AF.Ln, scale=1.0 / width, bias=epsc[:, 0:1]),
             reads=[bbuf(b), buf("epsc")], writes=[buf("lnt")])
        emit("act", lambda e: e.activation(out=rstd[:, :N], in_=lnt[:, :N], func=AF.Exp, scale=-0.5),
             reads=[buf("lnt")], writes=[buf("rstd")])
        for i, (sap, sbuf_) in enumerate(srcs):
            oap, obuf = outs[i]
            emit("dve", lambda e, sap=sap, oap=oap, g=gaps[i]: e.scalar_tensor_tensor(
                out=oap, in0=sap, scalar=g, in1=rstd[:, :N], op0=ALU.mult, op1=ALU.mult),
                reads=[sbuf_, buf("rstd"), buf("gcols")], writes=[obuf])

    def mm_group(N, b, lhs_list, rhs_list, reads, np_=128):
        n = len(lhs_list)
        for k in range(n):
            emit("pe", lambda e, k=k: e.matmul(bank(b)[:np_, :N], lhsT=lhs_list[k], rhs=rhs_list[k],
                                               start=(k == 0), stop=(k == n - 1)),
                 reads=reads, writes=[bbuf(b)])

    xbuf = [buf("xT%d" % i) for i in range(8)]
    hbuf = [buf("hT%d" % i) for i in range(8)]
    mixbuf = [buf("mix%d" % i) for i in range(8)]
    epsc = sb("epsc", [128, 1], F32)

    def process(kind, c):
        prompt = kind == "p"
        N = CH if prompt else DS
        NT = (N + 127) // 128
        np_ = min(N, 128)
        tok0 = c * CH if prompt else 0
        xsrc = xp if prompt else xs
        psrc = pp if prompt else ps_
        ydst = yp if prompt else ys
        nk_d = nkp if prompt else nks
        nv_d = nvp if prompt else nvs
        npool_d = npp if prompt else nps

        for tt in range(NT):
            emit("sp", lambda e, tt=tt: e.dma_start(out=xstg[:np_, :], in_=xsrc[tok0 + tt * 128: tok0 + tt * 128 + np_, :]),
                 writes=[buf("xstg")], dma_sem="xstg")
            for half in range(2):
                b = zrr.get()
                for q in range(4):
                    ft = half * 4 + q
                    emit("pe", lambda e, b=b, q=q, ft=ft: e.transpose(bank(b)[:, q * 128:q * 128 + np_],
                                                                      xstg[:np_, ft * 128:(ft + 1) * 128], ident32[:np_, :np_]),
                         reads=[buf("xstg"), buf("cmat32")], writes=[bbuf(b)])
                emit("dve", lambda e, b=b, half=half, tt=tt: e.tensor_copy(
                    out=xT[:, half * 4:half * 4 + 4, tt * 128:tt * 128 + np_],
                    in_=bank(b).rearrange("p (q n) -> p q n", q=4)[:, :, :np_]),
                    reads=[bbuf(b)], writes=xbuf[half * 4:half * 4 + 4])

        stage("load")
        for l in range(DEPTH):
            layer(kind, c, l, N, NT, np_, tok0, psrc, nk_d, nv_d, npool_d)

        youts = [((attT if i < 4 else praw)[:, i % 4, :N], buf(("att%d" if i < 4 else "praw%d") % (i % 4))) for i in range(8)]
        rmsnorm(N, [(xT[:, i, :N], xbuf[i]) for i in range(8)], [gcol(DEPTH, 0, i) for i in range(8)], youts, D)
        for tt in range(NT):
            for half in range(2):
                b = zrr.get()
                for q in range(4):
                    i = half * 4 + q
                    emit("pe", lambda e, b=b, q=q, i=i, tt=tt: e.transpose(bank(b)[:np_, q * 128:(q + 1) * 128],
                                                                           youts[i][0][:, tt * 128:tt * 128 + np_], ident32),
                         reads=[youts[i][1], buf("cmat32")], writes=[bbuf(b)])
                emit("dve", lambda e, b=b, half=half: e.tensor_copy(out=xstg[:np_, half * 512:(half + 1) * 512], in_=bank(b)[:np_, :]),
                     reads=[bbuf(b)], writes=[buf("xstg")])
            emit("sp", lambda e, tt=tt: e.dma_start(out=ydst[tok0 + tt * 128: tok0 + tt * 128 + np_, :], in_=xstg[:np_, :]),
                 reads=[buf("xstg")], dma_sem="xstg")

    def layer(kind, c, l, N, NT, np_, tok0, psrc, nk_d, nv_d, npool_d):
        prompt = kind == "p"
        rmsnorm(N, [(xT[:, i, :N], xbuf[i]) for i in range(8)], [gcol(l, 0, i) for i in range(8)],
                [(hT[:, i, :N], hbuf[i]) for i in range(8)], D)
        stage("norm1")
        ri, rb = load_slot(l, 0)
        W = ring[:, ri, :].rearrange("p (k n) -> p k n", k=8)
        for pr in range(4):
            b = zrr.get()
            mm_group(N, b, [W[:, k, pr * 128:(pr + 1) * 128] for k in range(8)], [hT[:, k, :N] for k in range(8)], [rb] + hbuf)
            emit("dve", lambda e, b=b, pr=pr: e.tensor_scalar(out=qT[:, pr, :N], in0=bank(b)[:, :N], scalar1=0.125, scalar2=None, op0=ALU.mult),
                 reads=[bbuf(b)], writes=[buf("qT%d" % pr)])
        stage("q")
        ri, rb = load_slot(l, 1)
        W = ring[:, ri, :].rearrange("p (k n) -> p k n", k=8)
        for pr in range(4):
            b = zrr.get()
            mm_group(N, b, [W[:, k, pr * 128:(pr + 1) * 128] for k in range(8)], [hT[:, k, :N] for k in range(8)], [rb] + hbuf)
            emit("act", lambda e, b=b, pr=pr: e.activation(out=kT[:, pr, :N], in_=bank(b)[:, :N], func=AF.Copy),
                 reads=[bbuf(b)], writes=[buf("kT%d" % pr)])
        if prompt:
            emit("sp", lambda e: e.dma_start(out=kts[l, :, :, tok0:tok0 + N].rearrange("a p s -> p a s"), in_=kT[:, :, :N]),
                 reads=[buf("kT%d" % pr) for pr in range(4)], writes=[buf("kts%d" % l)], dma_sem="ktst")
        for tt in range(NT):
            b = zrr.get()
            j = tt % 2
            mm_group(512, b, [hT[:, k, tt * 128:tt * 128 + np_] for k in range(8)], [W[:, k, :] for k in range(8)], [rb] + hbuf, np_=np_)
            emit("dve", lambda e, b=b, j=j: e.tensor_copy(out=stg[:np_, j, :], in_=bank(b)[:np_, :]),
                 reads=[bbuf(b)], writes=[buf("stg%d" % j)])
            emit("sp", lambda e, j=j, tt=tt: e.dma_start(
                out=nk_d[l, :, tok0 + tt * 128: tok0 + tt * 128 + np_, :].rearrange("h t d -> t h d"),
                in_=stg[:np_, j, :].rearrange("t (h d) -> t h d", h=8)),
                reads=[buf("stg%d" % j)], dma_sem="stg%d" % j)
        stage("k")
        ri, rb = load_slot(l, 2)
        W = ring[:, ri, :].rearrange("p (k n) -> p k n", k=8)
        for tt in range(NT):
            b = zrr.get()
            j = tt % 2
            mm_group(512, b, [hT[:, k, tt * 128:tt * 128 + np_] for k in range(8)], [W[:, k, :] for k in range(8)], [rb] + hbuf, np_=np_)
            emit("dve", lambda e, b=b, j=j: e.tensor_copy(out=stg[:np_, j, :], in_=bank(b)[:np_, :]),
                 reads=[bbuf(b)], writes=[buf("stg%d" % j)])
            emit("act", lambda e, j=j, tt=tt: e.activation(out=vtok[:np_, tt, :, 0:64], in_=stg[:np_, j, :].rearrange("t (h d) -> t h d", h=8), func=AF.Copy),
                 reads=[buf("stg%d" % j)], writes=[buf("vtok%d" % tt)])
            emit("sp", lambda e, j=j, tt=tt: e.dma_start(
                out=nv_d[l, :, tok0 + tt * 128: tok0 + tt * 128 + np_, :].rearrange("h t d -> t h d"),
                in_=stg[:np_, j, :].rearrange("t (h d) -> t h d", h=8)),
                reads=[buf("stg%d" % j)], dma_sem="stg%d" % j)
            if prompt:
                kb = tok0 // 128 + tt
                for a_ in range(4):
                    emit("sp", lambda e, tt=tt, kb=kb, a_=a_: e.dma_start(
                        out=vsc[l, a_, :, kb, :].rearrange("p (h d) -> p h d", h=2),
                        in_=vtok[:, tt, 2 * a_:2 * a_ + 2, 0:64]),
                        reads=[buf("vtok%d" % tt)], writes=[buf("vsc%d" % l)], dma_sem="vst")
        stage("v")
        ri, rb = load_slot(l, 3)
        W = ring[:, ri, :].rearrange("p (k n) -> p k n", k=8)
        if prompt:
            emit("pool", lambda e: e.tensor_copy(out=upx[:, :, 0:16], in_=khalo[:, l, :, :]),
                 reads=[buf("khalo%d" % l)], writes=[buf("upxh")])
        else:
            emit("pool", lambda e: e.memset(upx[:, :, 0:16], 0.0), writes=[buf("upxh")])
            for g in range(4):
                emit("sp", lambda e, g=g: e.dma_start(out=upx[:, g, 1:16], in_=stp[l, :, g * 128:(g + 1) * 128].rearrange("t p -> p t")),
                     writes=[buf("upxh")], dma_sem="upxh")
        for g in range(4):
            b = zrr.get()
            mm_group(N, b, [W[:, k, g * 128:(g + 1) * 128] for k in range(8)], [hT[:, k, :N] for k in range(8)], [rb] + hbuf)
            emit("act", lambda e, b=b, g=g: e.activation(out=upx[:, g, 16:16 + N], in_=bank(b)[:, :N], func=AF.Copy),
                 reads=[bbuf(b)], writes=[buf("upx%d" % g)])
        upall = [buf("upx%d" % g) for g in range(4)]
        if prompt:
            emit("pool", lambda e: e.tensor_copy(out=khalo[:, l, :, 1:16], in_=upx[:, :, 16 + N - 15:16 + N]),
                 reads=upall, writes=[buf("khalo%d" % l)])
        if (not prompt) or c == NCH - 1:
            for g in range(4):
                emit("sp", lambda e, g=g: e.dma_start(out=npool_d[l, :, g * 128:(g + 1) * 128].rearrange("t p -> p t"),
                                                      in_=upx[:, g, 16 + N - 15:16 + N]),
                     reads=[buf("upx%d" % g), buf("upxh")], dma_sem="npool")

        stage("up")
        L = 16 + N
        for g, w in enumerate(WINDOWS):
            src = upx[:, g, :]
            srcb = [buf("upx%d" % g), buf("upxh")]
            sh = 1
            for step in range(g + 1):
                dsti = step % 3
                dst = pt[:, dsti, :]
                lo = 2 * sh - 1
                emit("pool", lambda e, src=src, dst=dst, lo=lo, sh=sh: e.tensor_tensor(
                    out=dst[:, lo:L], in0=src[:, lo:L], in1=src[:, lo - sh:L - sh], op=ALU.add),
                    reads=srcb, writes=[buf("pt%d" % dsti)])
                src = dst
                srcb = [buf("pt%d" % dsti)]
                sh *= 2
            j = g % 2
            emit("dve", lambda e, src=src, g=g, j=j, w=w: e.scalar_tensor_tensor(
                out=dT[:, j, :N], in0=src[:, 16:16 + N], scalar=1.0 / w, in1=upx[:, g, 16:16 + N], op0=ALU.mult, op1=ALU.subtract),
                reads=srcb + [buf("upx%d" % g)], writes=[buf("dT%d" % j)])
            if prompt and c == 0:
                emit("dve", lambda e, src=src, g=g: e.tensor_tensor(out=tmp32[:, 0, 0:16], in0=src[:, 16:32], in1=invc[:, g, :], op=ALU.mult),
                     reads=srcb + [buf("invc")], writes=[buf("tmp0")])
                emit("dve", lambda e, g=g, j=j: e.tensor_tensor(out=dT[:, j, 0:16], in0=tmp32[:, 0, 0:16], in1=upx[:, g, 16:32], op=ALU.subtract),
                     reads=[buf("tmp0"), buf("upx%d" % g)], writes=[buf("dT%d" % j)])
            b = zrr.get()
            emit("pe", lambda e, b=b, g=g, j=j: e.matmul(bank(b)[:, :N], lhsT=wpool[:, 4 * l + g, :], rhs=dT[:, j, :N], start=True, stop=True),
                 reads=[buf("dT%d" % j), buf("wpool")], writes=[bbuf(b)])
            emit("act", lambda e, b=b, g=g: e.activation(out=praw[:, g, :N], in_=bank(b)[:, :N], func=AF.Copy),
                 reads=[bbuf(b)], writes=[buf("praw%d" % g)])
        rmsnorm(N, [(praw[:, g, :N], buf("praw%d" % g)) for g in range(4)], [gcol(l, 4, g) for g in range(4)],
                [(mixT[:, 4 + g, :N], mixbuf[4 + g]) for g in range(4)], PW)

        stage("pool")
        attention(kind, c, l, N, np_)
        rmsnorm(N, [(attT[:, g, :N], buf("att%d" % g)) for g in range(4)], [gcol(l, 3, g) for g in range(4)],
                [(mixT[:, g, :N], mixbuf[g]) for g in range(4)], 512)

        stage("att")
        for s in range(2):
            ri, rb = load_slot(l, 4 + s)
            W = ring[:, ri, :].rearrange("p (k n) -> p k n", k=8)
            for o in range(4):
                ot = s * 4 + o
                b = zrr.get()
                mm_group(N, b, [W[:, k, o * 128:(o + 1) * 128] for k in range(8)], [mixT[:, k, :N] for k in range(8)], [rb] + mixbuf)
                emit("dve", lambda e, b=b, ot=ot: e.tensor_tensor(out=xT[:, ot, :N], in0=xT[:, ot, :N], in1=bank(b)[:, :N], op=ALU.add),
                     reads=[bbuf(b)], writes=[xbuf[ot]])

        stage("outproj")
        rmsnorm(N, [(xT[:, i, :N], xbuf[i]) for i in range(8)], [gcol(l, 1, i) for i in range(8)],
                [(hT[:, i, :N], hbuf[i]) for i in range(8)], D)
        for i in range(11):
            ri, rb = load_slot(l, 6 + i)
            W = ring[:, ri, :].rearrange("p (k n) -> p k n", k=16)
            for j in range(2):
                ff = 2 * i + j
                bg = zrr.get()
                mm_group(N, bg, [W[:, k, j * 128:(j + 1) * 128] for k in range(8)], [hT[:, k, :N] for k in range(8)], [rb] + hbuf)
                bu = zrr.get()
                mm_group(N, bu, [W[:, 8 + k, j * 128:(j + 1) * 128] for k in range(8)], [hT[:, k, :N] for k in range(8)], [rb] + hbuf)
                jj = ff % 2
                emit("act", lambda e, bg=bg, jj=jj: e.activation(out=sgt[:, jj, :N], in_=bank(bg)[:, :N], func=AF.Silu),
                     reads=[bbuf(bg)], writes=[buf("sgt%d" % jj)])
                emit("dve", lambda e, bu=bu, jj=jj, ff=ff: e.tensor_tensor(out=actT[:, ff, :N], in0=sgt[:, jj, :N], in1=bank(bu)[:, :N], op=ALU.mult),
                     reads=[bbuf(bu), buf("sgt%d" % jj)], writes=[buf("act%d" % ff)])
        actbufs = [buf("act%d" % ff) for ff in range(22)]
        for ot in range(8):
            ri, rb = load_slot(l, 17 + ot, nelem=2816)
            W = ring[:, ri, 0:2816].rearrange("p (k n) -> p k n", k=22)
            b = zrr.get()
            mm_group(N, b, [W[:, k, :] for k in range(22)], [actT[:, k, :N] for k in range(22)], [rb] + actbufs)
            emit("dve", lambda e, b=b, ot=ot: e.tensor_tensor(out=xT[:, ot, :N], in0=xT[:, ot, :N], in1=bank(b)[:, :N], op=ALU.add),
                 reads=[bbuf(b)], writes=[xbuf[ot]])

        stage("ffn")
        rmsnorm(N, [(xT[:, i, :N], xbuf[i]) for i in range(8)], [gcol(l, 2, i) for i in range(8)],
                [(hT[:, i, :N], hbuf[i]) for i in range(8)], D)
        emit("sp", lambda e: e.dma_start(out=pstg[:np_, :NT, :], in_=psrc[l, tok0:tok0 + N, :].rearrange("(t p) f -> p t f", p=np_)),
             writes=[buf("pstg")], dma_sem="pstg")
        for k in range(2):
            b = zrr.get()
            for tt in range(NT):
                emit("pe", lambda e, b=b, k=k, tt=tt: e.transpose(bank(b)[:, tt * 128:tt * 128 + np_],
                                                                  pstg[:np_, tt, k * 128:(k + 1) * 128], ident32[:np_, :np_]),
                     reads=[buf("pstg"), buf("cmat32")], writes=[bbuf(b)])
            emit("act", lambda e, b=b, k=k: e.activation(out=pT[:, k, :N], in_=bank(b)[:, :N], func=AF.Copy),
                 reads=[bbuf(b)], writes=[buf("pT%d" % k)])
        rb2 = load_wple(l)
        Wp = wple_t[:, :].rearrange("p (k n) -> p k n", k=2)
        for s in range(2):
            ri, rb = load_slot(l, 25 + s)
            W = ring[:, ri, :].rearrange("p (k n) -> p k n", k=8)
            for o in range(4):
                ot = s * 4 + o
                bg = zrr.get()
                mm_group(N, bg, [W[:, k, o * 128:(o + 1) * 128] for k in range(8)], [hT[:, k, :N] for k in range(8)], [rb] + hbuf)
                bp = zrr.get()
                mm_group(N, bp, [Wp[:, k, ot * 128:(ot + 1) * 128] for k in range(2)], [pT[:, k, :N] for k in range(2)],
                         [rb2, buf("pT0"), buf("pT1")])
                jj = ot % 2
                emit("act", lambda e, bg=bg, jj=jj: e.activation(out=sgt[:, jj, :N], in_=bank(bg)[:, :N], func=AF.Sigmoid),
                     reads=[bbuf(bg)], writes=[buf("sgt%d" % jj)])
                emit("dve", lambda e, bp=bp, jj=jj: e.tensor_tensor(out=tmp32[:, jj, :N], in0=sgt[:, jj, :N], in1=bank(bp)[:, :N], op=ALU.mult),
                     reads=[bbuf(bp), buf("sgt%d" % jj)], writes=[buf("tmp%d" % jj)])
                emit("pool", lambda e, jj=jj, ot=ot: e.tensor_tensor(out=xT[:, ot, :N], in0=xT[:, ot, :N], in1=tmp32[:, jj, :N], op=ALU.add),
                     reads=[buf("tmp%d" % jj)], writes=[xbuf[ot]])

    att_state = {"i": 0, "seg": 0}

    def attention(kind, c, l, N, np_):
        for pr in range(4):
            attention_pair(kind, c, l, N, np_, pr)

    def attention_pair(kind, c, l, N, np_, pr):
        prompt = kind == "p"
        if True:
            groups = []
            if prompt:
                diag = []
                for j in (3, 2, 1, 0):
                    diag.append((lambda hh, j=j: kT[64 * hh:64 * hh + 64, pr, j * 128:(j + 1) * 128],
                                 lambda hh, j=j: vtok[:, j, 2 * pr + hh, :], 128, masks[:, j, :N],
                                 [buf("kT%d" % pr), buf("vtok%d" % j)]))
                groups.append((None, diag))
                nold = 4 * c
                nseg = (nold + segb - 1) // segb
                for sg_ in range(nseg - 1, -1, -1):
                    b0 = sg_ * segb
                    nb = min(segb, nold - b0)
                    groups.append(((b0, nb), None))
            else:
                diag = [(lambda hh: kT[64 * hh:64 * hh + 64, pr, 0:DS], lambda hh: vtok[:DS, 0, 2 * pr + hh, :], DS,
                         masks[:DS, 0, :DS], [buf("kT%d" % pr), buf("vtok0")])]
                groups.append((None, diag))
                groups.append((("cache", 8), None))

            tasks = []
            first = [True, True]
            seg_first = []
            seg_loaders = []
            for (segdesc, blocks) in groups:
                if blocks is None:
                    si = att_state["seg"] % 2
                    att_state["seg"] += 1
                    kb_ = buf("ktseg%d" % si)
                    vb_ = buf("vseg%d" % si)
                    if segdesc[0] == "cache":
                        nb = 8

                        def loader(si=si, kb_=kb_, vb_=vb_):
                            for kb in range(8):
                                emit("sp", lambda e, kb=kb: e.dma_start(
                                    out=pstg[:, 0, 0:128].rearrange("p (h d) -> p h d", h=2),
                                    in_=ck[l, 2 * pr:2 * pr + 2, kb * 128:(kb + 1) * 128, :].rearrange("h k d -> k h d")),
                                    writes=[buf("pstg")], dma_sem="pstg")
                                q4 = kb % 4
                                if q4 == 0:
                                    bq = zrr.get()
                                emit("pe", lambda e, bq=bq, q4=q4: e.transpose(bank(bq)[:, q4 * 128:(q4 + 1) * 128], pstg[:, 0, 0:128], ident32),
                                     reads=[buf("pstg"), buf("cmat32")], writes=[bbuf(bq)])
                                if q4 == 3:
                                    emit("dve", lambda e, bq=bq, kb=kb, si=si: e.tensor_copy(out=ktseg[:, si, (kb - 3) * 128:(kb + 1) * 128], in_=bank(bq)[:, :]),
                                         reads=[bbuf(bq)], writes=[kb_])
                            for half in range(2):
                                for h2 in range(2):
                                    emit("sp", lambda e, half=half, h2=h2: e.dma_start(
                                        out=xstg[:, 0:512].rearrange("p (k h d) -> p k h d", k=4, h=2)[:, :, h2, :],
                                        in_=cv[l, 2 * pr + h2, half * 512:(half + 1) * 512, :].rearrange("(k p) d -> p k d", p=128)),
                                        writes=[buf("xstg")], dma_sem="xstg")
                                for h2 in range(2):
                                    emit("dve", lambda e, half=half, si=si, h2=h2: e.tensor_copy(
                                        out=vseg[:, si, half * 4:(half + 1) * 4, h2, 0:64],
                                        in_=xstg[:, 0:512].rearrange("p (k h d) -> p k h d", k=4, h=2)[:, :, h2, :]),
                                        reads=[buf("xstg")], writes=[vb_])
                    else:
                        b0, nb = segdesc

                        def loader(si=si, b0=b0, nb=nb, kb_=kb_, vb_=vb_):
                            emit("sp", lambda e: e.dma_start(out=ktseg[:, si, 0:nb * 128], in_=kts[l, pr, :, b0 * 128:(b0 + nb) * 128]),
                                 reads=[buf("kts%d" % l)], writes=[kb_], dma_sem="ktseg%d" % si)
                            for h2 in range(2):
                                emit("sp", lambda e, h2=h2: e.dma_start(out=vseg[:, si, 0:nb, h2, 0:64], in_=vsc[l, pr, :, b0:b0 + nb, h2 * 64:(h2 + 1) * 64]),
                                     reads=[buf("vsc%d" % l)], writes=[vb_], dma_sem="vseg%d" % si)
                    seg_first.append(len(tasks))
                    ns_ = len(seg_first)
                    trig = 0 if ns_ <= 2 else seg_first[ns_ - 2] + 4
                    seg_loaders.append((trig, loader))
                    blocks = []
                    for kk in range(nb - 1, -1, -1):
                        blocks.append((lambda hh, kk=kk, si=si: ktseg[64 * hh:64 * hh + 64, si, kk * 128:(kk + 1) * 128],
                                       lambda hh, kk=kk, si=si: vseg[:, si, kk, hh, :], 128, None, [kb_, vb_]))
                for blk in blocks:
                    for hh in range(2):
                        tasks.append((blk, hh))

            ntask = len(tasks)
            last_idx = [max(i for i in range(ntask) if tasks[i][1] == hh) for hh in range(2)]
            st = [dict() for _ in range(ntask)]
            ob = [0, 1]

            def stage1(i):
                (ktf, vap, nk, mk, rds), hh = tasks[i]
                zb = zrr.get()
                ii = att_state["i"]
                att_state["i"] += 1
                st[i].update(zb=zb, e=ii % 2, s=ii % 3, a=ii % 3)
                emit("pe", lambda e: e.matmul(bank(zb)[:nk, :N], lhsT=ktf(hh), rhs=qT[64 * hh:64 * hh + 64, pr, :N], start=True, stop=False),
                     reads=[rds[0], buf("qT%d" % pr)], writes=[bbuf(zb)])
                ei, si_ = st[i]["e"], st[i]["s"]
                emit("act", lambda e: e.activation(out=e32[:nk, ei, :N], in_=bank(zb)[:nk, :N], func=AF.Exp),
                     reads=[bbuf(zb)], writes=[buf("e32_%d" % ei)])
                st[i]["nk"] = nk

            def stage1b(i):
                (ktf, vap, nk, mk, rds), hh = tasks[i]
                ei, si_ = st[i]["e"], st[i]["s"]
                emit("act", lambda e: e.activation(out=spb[:nk, si_, :N], in_=e32[:nk, ei, :N], func=AF.Ln, bias=onec[:nk, 0:1], scale=1.0),
                     reads=[buf("e32_%d" % ei), buf("epsc")], writes=[buf("spb%d" % si_)])
                if mk is not None:
                    emit("pool", lambda e: e.tensor_tensor(out=spb[:nk, si_, :N], in0=spb[:nk, si_, :N], in1=mk[:nk], op=ALU.mult),
                         reads=[buf("masks")], writes=[buf("spb%d" % si_)])

            NQT = (N + 127) // 128
            npq = min(N, 128)
            accb = [buf("accA"), buf("accB")]
            emit("pool", lambda e: e.memset(acc[:], 0.0), writes=accb)
            emit("dve", lambda e: e.memset(Rst[:], 1.0), writes=[buf("Rst0"), buf("Rst1")])

            def stage2(i):
                (ktf, vapf, nk, mk, rds), hh = tasks[i]
                zb, si_ = st[i]["zb"], st[i]["s"]
                emit("pe", lambda e: e.matmul(bank(zb)[:nk, :N], lhsT=negtri[:nk, :nk], rhs=spb[:nk, si_, :N], start=False, stop=True),
                     reads=[buf("spb%d" % si_), buf("cmatb")], writes=[bbuf(zb)])

            def stage3(i):
                (ktf, vapf, nk, mk, rds), hh = tasks[i]
                zb, ai = st[i]["zb"], st[i]["a"]
                emit("act", lambda e: e.activation(out=ab[:nk, ai, :N], in_=bank(zb)[:nk, :N], func=AF.Exp),
                     reads=[bbuf(zb)], writes=[buf("ab%d" % ai)])
                if mk is not None:
                    emit("dve", lambda e: e.tensor_tensor(out=ab[:nk, ai, :N], in0=ab[:nk, ai, :N], in1=mk[:nk], op=ALU.mult),
                         reads=[buf("masks")], writes=[buf("ab%d" % ai)])

            def stage4(i):
                (ktf, vapf, nk, mk, rds), hh = tasks[i]
                ai = st[i]["a"]
                pb = i % 2
                pj = i % 2
                for qt in range(NQT):
                    emit("pe", lambda e, qt=qt: e.matmul(bank(pb)[:npq, qt * 65:(qt + 1) * 65], lhsT=ab[:nk, ai, qt * 128:qt * 128 + npq],
                                                       rhs=vapf(hh), start=True, stop=True),
                         reads=[buf("ab%d" % ai), rds[1]], writes=[bbuf(pb)])
                P = bank(pb)[:npq, 0:NQT * 65].rearrange("p (q c) -> p q c", c=65)
                emit("dve", lambda e: e.tensor_tensor(out=tmpP[:npq, pj, :NQT, :], in0=P[:, :, 0:64],
                                                      in1=Rst[:npq, hh, :NQT].unsqueeze(2).to_broadcast([npq, NQT, 64]), op=ALU.mult),
                     reads=[bbuf(pb), buf("Rst%d" % hh)], writes=[buf("tmpP%d" % pj)])
                emit("pool", lambda e: e.tensor_tensor(out=acc[:npq, :NQT, 64 * hh:64 * hh + 64], in0=acc[:npq, :NQT, 64 * hh:64 * hh + 64],
                                                       in1=tmpP[:npq, pj, :NQT, :], op=ALU.add),
                     reads=[buf("tmpP%d" % pj)], writes=[accb[hh]])
                if i != last_idx[hh]:
                    emit("dve", lambda e: e.tensor_scalar(out=mmt[:npq, hh, :NQT], in0=P[:, :, 64], scalar1=-1.0, scalar2=1.0, op0=ALU.mult, op1=ALU.add),
                         reads=[bbuf(pb)], writes=[buf("mmt%d" % hh)])
                    emit("dve", lambda e: e.scalar_tensor_tensor(out=Rst[:npq, hh, :NQT], in0=mmt[:npq, hh, :NQT], scalar=0.0, in1=Rst[:npq, hh, :NQT],
                                                                 op0=ALU.max, op1=ALU.mult),
                         reads=[buf("mmt%d" % hh)], writes=[buf("Rst%d" % hh)])
                if i == max(last_idx):
                    bq_ = zrr.get()
                    for qt in range(NQT):
                        emit("pe", lambda e, qt=qt: e.transpose(bank(bq_)[:, qt * 128:qt * 128 + npq], acc[:npq, qt, :], ident32[:npq, :npq]),
                             reads=accb + [buf("cmat32")], writes=[bbuf(bq_)])
                    emit("act", lambda e: e.activation(out=attT[:, pr, :N], in_=bank(bq_)[:, :N], func=AF.Copy),
                         reads=[bbuf(bq_)], writes=[buf("att%d" % pr)])

            for step in range(ntask + 3):
                for (trig, fn_) in seg_loaders:
                    if trig == step:
                        fn_()
                if step < ntask:
                    stage1(step)
                if 0 <= step - 2 < ntask:
                    stage3(step - 2)
                if step < ntask:
                    stage1b(step)
                if 0 <= step - 1 < ntask:
                    stage2(step - 1)
                if 0 <= step - 3 < ntask:
                    stage4(step - 3)

    onec = sb("onec", [128, 1], F32)

    emit("dve", lambda e: e.memset(epsc[:], EPS), writes=[buf("epsc")])
    emit("dve", lambda e: e.memset(onec[:], 1.0), writes=[buf("epsc")])
    prologue()
    stage("prologue")
    try:
        if DO_SAMPLE:
            process("s", 0)
        for c in range(NCH):
            process("p", c)
    except _Stop:
        pass

    final_waits = [((k, 0), v) for k, v in sc.dsem.items()]

    semv = {}
    ptr = {e: 0 for e in Sched.ENGS}
    progress = True
    while progress:
        progress = False
        for e_ in Sched.ENGS:
            ol = sc.ops[e_]
            while ptr[e_] < len(ol):
                waits, fn, inc = ol[ptr[e_]]
                if all(semv.get(k, 0) >= v for (k, v) in waits):
                    semv[inc[0]] = semv.get(inc[0], 0) + inc[1]
                    ptr[e_] += 1
                    progress = True
                else:
                    break
    for e_ in Sched.ENGS:
        if ptr[e_] < len(sc.ops[e_]):
            waits, fn, inc = sc.ops[e_][ptr[e_]]
            raise RuntimeError("schedule deadlock: %s stuck at op %d waits %s have %s" % (
                e_, ptr[e_], waits, [(k, semv.get(k, 0)) for k, v in waits]))

    sem_names = set()
    for eng in Sched.ENGS:
        for (waits, fn, inc) in sc.ops[eng]:
            sem_names.add(inc[0])
            for (k, v) in waits:
                sem_names.add(k)
    for k, v in final_waits:
        sem_names.add(k)
    sem_names = sorted(sem_names, key=str)
    from contextlib import ExitStack
    with ExitStack() as es:
        sems = {}
        for k in sem_names:
            sems[k] = es.enter_context(nc.semaphore("s_%s_%d" % (k[0], k[1])))
        es.enter_context(nc.allow_non_contiguous_dma(reason="small transposing state DMAs"))
        block = es.enter_context(nc.Block())

        def runner(engname):
            def f(e):
                for (waits, fn, inc) in sc.ops[engname]:
                    for (k, v) in waits:
                        e.wait_ge(sems[k], v)
                    ins = fn(e)
                    ins.then_inc(sems[inc[0]], inc[1])
                if engname == "sp":
                    for (k, v) in final_waits:
                        e.wait_ge(sems[k], v)
            return f

        block.tensor(runner("pe"))
        block.scalar(runner("act"))
        block.vector(runner("dve"))
        block.gpsimd(runner("pool"))
        block.sync(runner("sp"))
    return nc, {e: len(sc.ops[e]) for e in Sched.ENGS}


def _consts():
    p = np.arange(128)[:, None]
    t = np.arange(512)[None, :]
    masks = np.zeros((128, 4, 512), np.float32)
    for j in range(4):
        masks[:, j, :] = ((128 * j + p) < t).astype(np.float32)
    invc = np.zeros((128, 4, 16), np.float32)
    for g, w in enumerate(WINDOWS):
        invc[:, g, :] = 1.0 / np.minimum(w, np.arange(16) + 1).astype(np.float32)[None, :]
    cm = np.zeros((128, 4, 128), np.float32)
    cm[:, 0, :] = np.eye(128, dtype=np.float32)
    j = np.arange(128)[:, None]
    s_ = np.arange(128)[None, :]
    cm[:, 1, :] = -(j >= s_).astype(np.float32)
    cm[:, 2, :] = -1.0
    cm[:, 3, :] = 1.0
    return masks, invc, cm


def _gcols(g_mix, g_ffn, g_ple, g_sb_out, pool_scale, g_final):
    out = np.zeros((128, 136), np.float32)
    for l in range(DEPTH):
        out[:, 32 * l + 0:32 * l + 8] = g_mix[l].reshape(8, 128).T
        out[:, 32 * l + 8:32 * l + 16] = g_ffn[l].reshape(8, 128).T
        out[:, 32 * l + 16:32 * l + 24] = g_ple[l].reshape(8, 128).T
        out[:, 32 * l + 24:32 * l + 28] = g_sb_out[l].reshape(4, 128).T
        out[:, 32 * l + 28:32 * l + 32] = pool_scale[l].reshape(4, 128).T
    out[:, 128:136] = g_final.reshape(8, 128).T
    return out


_CACHE = {}


def kernel(x_prompt, x_sample, cache_k, cache_v, state_pool, p_prompt, p_sample,
           g_mix, w_in, g_sb_out, w_pool, pool_scale, w_out,
           g_ffn, w_ffn_gate, w_ffn_up, w_ffn_down, g_ple, w_ple, w_ple_gate, g_final,
           _nch=16, _sample=True, _cores=8):
    f = lambda a: np.ascontiguousarray(np.asarray(a, dtype=np.float32))
    key = (_nch, _sample)
    if key not in _CACHE:
        _CACHE[key] = build(_nch, _sample)
    nc, counts = _CACHE[key]
    masks, invc, cm = _consts()
    gc = _gcols(f(g_mix), f(g_ffn), f(g_ple), f(g_sb_out), f(pool_scale), f(g_final))
    shared = dict(w_in=f(w_in), w_pool=f(w_pool), w_out=f(w_out), w_g=f(w_ffn_gate), w_u=f(w_ffn_up), w_d=f(w_ffn_down),
                  w_ple=f(w_ple), w_pg=f(w_ple_gate), gcols=gc, masks=masks, invc=invc, cmat=cm)
    x_prompt = f(x_prompt); p_prompt = f(p_prompt); x_sample = f(x_sample); p_sample = f(p_sample)
    cache_k = f(cache_k); cache_v = f(cache_v); state_pool = f(state_pool)
    in_maps = []
    for c in range(8):
        b = c % NBATCH
        m = dict(shared)
        m.update(xp=x_prompt[b], pp=np.ascontiguousarray(p_prompt[:, b]), xs=x_sample[c],
                 ps=np.ascontiguousarray(p_sample[:, c]), ck=np.ascontiguousarray(cache_k[:, c]),
                 cv=np.ascontiguousarray(cache_v[:, c]), stp=np.ascontiguousarray(state_pool[:, c]))
        in_maps.append(m)
    res = run_bass_kernel_spmd(nc, in_maps[:_cores], core_ids=list(range(_cores)))
    r = list(res.results) + [res.results[0]] * (8 - _cores)
    y_prompt = np.stack([r[b]["yp"] for b in range(NBATCH)])
    y_sample = np.stack([r[c]["ys"] for c in range(8)])
    nk_p = np.stack([r[b]["nkp"] for b in range(NBATCH)], axis=1)
    nv_p = np.stack([r[b]["nvp"] for b in range(NBATCH)], axis=1)
    np_p = np.stack([r[b]["npp"] for b in range(NBATCH)], axis=1)
    nk_s = np.stack([r[c]["nks"] for c in range(8)], axis=1)
    nv_s = np.stack([r[c]["nvs"] for c in range(8)], axis=1)
    np_s = np.stack([r[c]["nps"] for c in range(8)], axis=1)
    return (y_prompt.astype(np.float32), y_sample.astype(np.float32), nk_p.astype(np.float32), nv_p.astype(np.float32),
            np_p.astype(np.float32), nk_s.astype(np.float32), nv_s.astype(np.float32), np_s.astype(np.float32))
```

```python
import numpy as np
import concourse.bass as bass
import concourse.mybir as mybir
from concourse.bass_utils import run_bass_kernel_spmd

F32 = mybir.dt.float32
BF16 = mybir.dt.bfloat16
AF = mybir.ActivationFunctionType
ALU = mybir.AluOpType

D = 1024
DEPTH = 4
S = 8192
NBATCH = 4
H = 8
DH = 64
PW = 512
DFF = 2816
PLE = 256
CH = 512
DS = 32
PAST = 1024
EPS = 1e-6
NSLOT = 28
NRING = 4
SEGB = 16
EPOCH = 12000
NEPOCH = 24
WINDOWS = (2, 4, 8, 16)


class Buf:
    __slots__ = ("w", "r", "const")

    def __init__(self, const=False):
        self.w = None
        self.r = {}
        self.const = const


class Sched:
    ENGS = ("pe", "act", "dve", "pool", "sp")

    def __init__(self):
        self.ops = {e: [] for e in self.ENGS}
        self.cnt = {e: 0 for e in self.ENGS}
        self.known = {e: {} for e in self.ENGS}
        self.dsem = {}

    def emit(self, eng, fn, reads=(), writes=(), extra=(), dma_sem=None):
        deps = {}

        def add(ev):
            if ev is None:
                return
            k, v = ev
            if deps.get(k, 0) < v:
                deps[k] = v

        for b in reads:
            add(b.w)
        for b in writes:
            add(b.w)
            for k, v in b.r.items():
                add((k, v))
        for ev in extra:
            add(ev)
        waits = []
        kn = self.known[eng]
        for k, v in deps.items():
            if eng == "pe" and k[0] == "pe":
                continue
            if kn.get(k, 0) >= v:
                continue
            kn[k] = v
            waits.append((k, v))
        if dma_sem is not None:
            self.dsem[dma_sem] = self.dsem.get(dma_sem, 0) + 16
            ev = ((dma_sem, 0), self.dsem[dma_sem])
            inc = ((dma_sem, 0), 16)
        else:
            c = self.cnt[eng]
            self.cnt[eng] = c + 1
            ev = ((eng, c // EPOCH), c % EPOCH + 1)
            inc = (ev[0], 1)
        self.ops[eng].append((waits, fn, inc))
        for b in reads:
            if not b.const:
                if b.r.get(ev[0], 0) < ev[1]:
                    b.r[ev[0]] = ev[1]
        for b in writes:
            b.w = ev
            b.r = {}
        return ev


class _Stop(Exception):
    pass


def build(NCH=16, DO_SAMPLE=True, STOP=None, NOCONV=False, segb=SEGB):
    nc = bass.Bass("TRN2", target_bir_lowering=False)

    stage_cnt = {}

    def stage(name):
        stage_cnt[name] = stage_cnt.get(name, 0) + 1
        if STOP is not None:
            sn, _, sk = STOP.partition(":")
            if sn == name and stage_cnt[name] == int(sk or 1):
                raise _Stop()
    sc = Sched()

    def din(name, shape, dt=F32):
        return nc.dram_tensor(name, list(shape), dt, kind="ExternalInput").ap()

    def dout(name, shape, dt=F32):
        return nc.dram_tensor(name, list(shape), dt, kind="ExternalOutput").ap()

    def dint(name, shape, dt):
        return nc.dram_tensor(name, list(shape), dt, kind="Internal").ap()

    xp = din("xp", [S, D])
    pp = din("pp", [DEPTH, S, PLE])
    xs = din("xs", [DS, D])
    ps_ = din("ps", [DEPTH, DS, PLE])
    ck = din("ck", [DEPTH, H, PAST, DH])
    cv = din("cv", [DEPTH, H, PAST, DH])
    stp = din("stp", [DEPTH, 15, PW])
    w_in = din("w_in", [DEPTH, D, 2048])
    w_pool = din("w_pool", [DEPTH, 4, 128, 128])
    w_out = din("w_out", [DEPTH, D, D])
    w_g = din("w_g", [DEPTH, D, DFF])
    w_u = din("w_u", [DEPTH, D, DFF])
    w_d = din("w_d", [DEPTH, DFF, D])
    w_ple = din("w_ple", [DEPTH, PLE, D])
    w_pg = din("w_pg", [DEPTH, D, D])
    gcols_d = din("gcols", [128, 136])
    masks_d = din("masks", [128, 4, 512])
    invc_d = din("invc", [128, 4, 16])
    cmat_d = din("cmat", [128, 4, 128])

    yp = dout("yp", [S, D])
    ys = dout("ys", [DS, D])
    nkp = dout("nkp", [DEPTH, H, S, DH])
    nvp = dout("nvp", [DEPTH, H, S, DH])
    npp = dout("npp", [DEPTH, 15, PW])
    nks = dout("nks", [DEPTH, H, DS, DH])
    nvs = dout("nvs", [DEPTH, H, DS, DH])
    nps = dout("nps", [DEPTH, 15, PW])

    wscr = dint("wscr", [DEPTH, NSLOT, 128, 4096], BF16)
    kts = dint("kts", [DEPTH, 4, 128, S], BF16)
    vsc = dint("vsc", [DEPTH, 4, 128, S // 128, 128], BF16)

    def sb(name, shape, dt):
        return nc.alloc_sbuf_tensor("sb_" + name, list(shape), dt)

    xT = sb("xT", [128, 8, CH], F32)
    hT = sb("hT", [128, 8, CH], BF16)
    sq = sb("sq", [128, 2, CH], BF16)
    rstd = sb("rstd", [128, CH], F32)
    lnt = sb("lnt", [128, CH], F32)
    qT = sb("qT", [128, 4, CH], BF16)
    kT = sb("kT", [128, 4, CH], BF16)
    vtok = sb("vtok", [128, 4, 8, 65], BF16)
    stg = sb("stg", [128, 2, 512], F32)
    upx = sb("upx", [128, 4, 16 + CH], F32)
    pt = sb("pt", [128, 3, 16 + CH], F32)
    dT = sb("dT", [128, 2, CH], BF16)
    praw = sb("praw", [128, 4, CH], F32)
    attT = sb("attT", [128, 4, CH], F32)
    mixT = sb("mixT", [128, 8, CH], BF16)
    e32 = sb("e32", [128, 2, CH], F32)
    spb = sb("spb", [128, 3, CH], BF16)
    acc = sb("acc", [128, 4, 128], F32)
    tmpP = sb("tmpP", [128, 2, 4, 64], F32)
    Rst = sb("Rst", [128, 2, 4], F32)
    mmt = sb("mmt", [128, 2, 4], F32)
    ab = sb("ab", [128, 3, CH], BF16)
    ktseg = sb("ktseg", [128, 2, SEGB * 128], BF16)
    vseg = sb("vseg", [128, 2, SEGB, 2, 65], BF16)
    actT = sb("actT", [128, 22, CH], BF16)
    ring = sb("ring", [128, NRING, 4096], BF16)
    xstg = sb("xstg", [128, 1024], F32)
    pstg = sb("pstg", [128, 4, 256], F32)
    pT = sb("pT", [128, 2, CH], BF16)
    masks = sb("masks", [128, 4, 512], BF16)
    wpool = sb("wpool", [128, 16, 128], BF16)
    gcols = sb("gcols", [128, 136], F32)
    invc = sb("invc", [128, 4, 16], F32)
    cmat32 = sb("cmat32", [128, 4, 128], F32)
    cmatb = sb("cmatb", [128, 4, 128], BF16)
    khalo = sb("khalo", [128, DEPTH, 4, 16], F32)
    sgt = sb("sgt", [128, 2, CH], F32)
    tmp32 = sb("tmp32", [128, 2, CH], F32)
    psum = nc.alloc_psum_tensor("psum", [128, 8, 512], F32)

    B = {}

    def buf(name, const=False):
        if name not in B:
            B[name] = Buf(const)
        return B[name]

    ident32 = cmat32[:, 0, :]
    negtri = cmatb[:, 1, :]
    negones = cmatb[:, 2, :]
    ones = cmatb[:, 3, :]
    CONST = buf("const", True)

    class BankRR:
        def __init__(self, ids):
            self.ids = ids
            self.i = 0

        def get(self):
            b = self.ids[self.i % len(self.ids)]
            self.i += 1
            return b

    zrr = BankRR([2, 3, 4, 5, 6, 7])

    def bank(b):
        return psum[:, b, :]

    def bbuf(b):
        return buf("bank%d" % b)

    emit = sc.emit

    ring_state = {"n": 0}
    ring_bufs = [buf("ring%d" % i) for i in range(NRING)]
    wconv_bufs = [buf("wconv%d" % l) for l in range(DEPTH)]

    converted = set()

    def slot_pieces(l, s):
        r = lambda ap: ap.rearrange("(k p) n -> p k n", p=128)
        if s < 4:
            return [(0, 8, 512, r(w_in[l, :, 512 * s:512 * s + 512]))]
        if s < 6:
            return [(0, 8, 512, r(w_out[l, :, 512 * (s - 4):512 * (s - 4) + 512]))]
        if s < 17:
            i = s - 6
            return [(0, 8, 256, r(w_g[l, :, 256 * i:256 * i + 256])), (2048, 8, 256, r(w_u[l, :, 256 * i:256 * i + 256]))]
        if s < 25:
            j = s - 17
            return [(0, 22, 128, r(w_d[l, :, 128 * j:128 * j + 128]))]
        if s < 27:
            return [(0, 8, 512, r(w_pg[l, :, 512 * (s - 25):512 * (s - 25) + 512]))]
        return [(0, 2, 1024, r(w_ple[l, :, :]))]

    SLOT_ORDER = list(range(0, 27))
    SLOT_NELEM = {s_: (2816 if 17 <= s_ < 25 else (2048 if s_ == 27 else 4096)) for s_ in range(NSLOT)}
    ngroups = NCH + (1 if DO_SAMPLE else 0)
    slot_plan = [(l_, s_) for _g in range(ngroups) for l_ in range(DEPTH) for s_ in SLOT_ORDER]
    plan_state = {"issued": 0, "used": 0}
    PF = NRING - 1

    def issue_slot(n):
        l, s = slot_plan[n]
        nelem = SLOT_NELEM[s]
        i = n % NRING
        rb = ring_bufs[i]
        wsb = buf("wscr_%d_%d" % (l, s))
        if (l, s) not in converted:
            converted.add((l, s))
            for (off, kt, ncol, src) in slot_pieces(l, s):
                emit("pool", lambda e, i=i, off=off, kt=kt, ncol=ncol, src=src: e.dma_start(
                    out=ring[:, i, off:off + kt * ncol].rearrange("p (k n) -> p k n", k=kt), in_=src),
                    writes=[rb], dma_sem="ring%d" % i)
            emit("sp", lambda e, l=l, s=s, i=i, nelem=nelem: e.dma_start(out=wscr[l, s, :, 0:nelem], in_=ring[:, i, 0:nelem]),
                 reads=[rb], writes=[wsb], dma_sem="wst")
        else:
            emit("sp", lambda e, l=l, s=s, i=i, nelem=nelem: e.dma_start(out=ring[:, i, 0:nelem], in_=wscr[l, s, :, 0:nelem]),
                 reads=[wsb], writes=[rb], dma_sem="ring%d" % i)

    wple_t = sb("wple_t", [128, 2048], BF16)

    def load_wple(l):
        wb_ = buf("wple_t")
        wsb = buf("wscr_%d_%d" % (l, 27))
        if (l, 27) not in converted:
            converted.add((l, 27))
            for (off, kt, ncol, src) in slot_pieces(l, 27):
                emit("pool", lambda e, src=src: e.dma_start(out=wple_t[:, :].rearrange("p (k n) -> p k n", k=2), in_=src),
                     writes=[wb_], dma_sem="wple")
            emit("sp", lambda e: e.dma_start(out=wscr[l, 27, :, 0:2048], in_=wple_t[:, :]),
                 reads=[wb_], writes=[wsb], dma_sem="wst")
        else:
            emit("sp", lambda e: e.dma_start(out=wple_t[:, :], in_=wscr[l, 27, :, 0:2048]),
                 reads=[wsb], writes=[wb_], dma_sem="wple")
        return wb_

    def load_slot(l, s, nelem=4096):
        n = plan_state["used"]
        assert slot_plan[n] == (l, s), (slot_plan[n], l, s)
        plan_state["used"] += 1
        while plan_state["issued"] < min(len(slot_plan), n + PF + 1):
            issue_slot(plan_state["issued"])
            plan_state["issued"] += 1
        return n % NRING, ring_bufs[n % NRING]

    def prologue():
        emit("sp", lambda e: e.dma_start(out=gcols[:], in_=gcols_d[:, :]), writes=[buf("gcols")], dma_sem="c0")
        emit("sp", lambda e: e.dma_start(out=invc[:], in_=invc_d[:, :, :]), writes=[buf("invc")], dma_sem="c1")
        emit("sp", lambda e: e.dma_start(out=cmat32[:], in_=cmat_d[:, :, :]), writes=[buf("cmat32")], dma_sem="c2")
        emit("pool", lambda e: e.dma_start(out=cmatb[:], in_=cmat_d[:, :, :]), writes=[buf("cmatb")], dma_sem="c3")
        emit("pool", lambda e: e.dma_start(out=masks[:], in_=masks_d[:, :, :]), writes=[buf("masks")], dma_sem="c4")
        for l in range(DEPTH):
            emit("pool", lambda e, l=l: e.dma_start(out=wpool[:, 4 * l:4 * l + 4, :],
                                                    in_=w_pool[l].rearrange("g c d -> c g d")),
                 writes=[buf("wpool")], dma_sem="c5")
        emit("dve", lambda e: e.memset(khalo[:], 0.0), writes=[buf("khalo%d" % l) for l in range(DEPTH)])
        emit("dve", lambda e: e.memset(vtok[:, :, :, 64:65], 1.0), writes=[buf("vtok%d" % t_) for t_ in range(4)])
        emit("dve", lambda e: e.memset(vseg[:, :, :, :, 64:65].rearrange("p a b c d -> p (a b c) d"), 1.0), writes=[buf("vseg0"), buf("vseg1")])

    def gcol(l, which, i):
        if l == DEPTH:
            c = 128 + i
        else:
            c = 32 * l + (0, 8, 16, 24, 28)[which] + i
        return gcols[:, c:c + 1]

    def rmsnorm(N, srcs, gaps, outs, width):
        nt = len(srcs)
        b = zrr.get()
        for i, (sap, sbuf_) in enumerate(srcs):
            j = i % 2
            emit("act", lambda e, sap=sap, j=j: e.activation(out=sq[:, j, :N], in_=sap, func=AF.Square),
                 reads=[sbuf_], writes=[buf("sq%d" % j)])
            emit("pe", lambda e, b=b, j=j, i=i: e.matmul(bank(b)[:, :N], lhsT=ones, rhs=sq[:, j, :N],
                                                         start=(i == 0), stop=(i == nt - 1)),
                 reads=[buf("sq%d" % j), buf("cmatb")], writes=[bbuf(b)])
        emit("act", lambda e, b=b: e.activation(out=lnt[:, :N], in_=bank(b)[:, :N], func=AF.Ln, scale=1.0 / width, bias=epsc[:, 0:1]),
             reads=[bbuf(b), buf("epsc")], writes=[buf("lnt")])
        emit("act", lambda e: e.activation(out=rstd[:, :N], in_=lnt[:, :N], func=AF.Exp, scale=-0.5),
             reads=[buf("lnt")], writes=[buf("rstd")])
        for i, (sap, sbuf_) in enumerate(srcs):
            oap, obuf = outs[i]
            emit("dve", lambda e, sap=sap, oap=oap, g=gaps[i]: e.scalar_tensor_tensor(
                out=oap, in0=sap, scalar=g, in1=rstd[:, :N], op0=ALU.mult, op1=ALU.mult),
                reads=[sbuf_, buf("rstd"), buf("gcols")], writes=[obuf])

    def mm_group(N, b, lhs_list, rhs_list, reads, np_=128):
        n = len(lhs_list)
        for k in range(n):
            emit("pe", lambda e, k=k: e.matmul(bank(b)[:np_, :N], lhsT=lhs_list[k], rhs=rhs_list[k],
                                               start=(k == 0), stop=(k == n - 1)),
                 reads=reads, writes=[bbuf(b)])

    xbuf = [buf("xT%d" % i) for i in range(8)]
    hbuf = [buf("hT%d" % i) for i in range(8)]
    mixbuf = [buf("mix%d" % i) for i in range(8)]
    epsc = sb("epsc", [128, 1], F32)

    def process(kind, c):
        prompt = kind == "p"
        N = CH if prompt else DS
        NT = (N + 127) // 128
        np_ = min(N, 128)
        tok0 = c * CH if prompt else 0
        xsrc = xp if prompt else xs
        psrc = pp if prompt else ps_
        ydst = yp if prompt else ys
        nk_d = nkp if prompt else nks
        nv_d = nvp if prompt else nvs
        npool_d = npp if prompt else nps

        for tt in range(NT):
            emit("sp", lambda e, tt=tt: e.dma_start(out=xstg[:np_, :], in_=xsrc[tok0 + tt * 128: tok0 + tt * 128 + np_, :]),
                 writes=[buf("xstg")], dma_sem="xstg")
            for half in range(2):
                b = zrr.get()
                for q in range(4):
                    ft = half * 4 + q
                    emit("pe", lambda e, b=b, q=q, ft=ft: e.transpose(bank(b)[:, q * 128:q * 128 + np_],
                                                                      xstg[:np_, ft * 128:(ft + 1) * 128], ident32[:np_, :np_]),
                         reads=[buf("xstg"), buf("cmat32")], writes=[bbuf(b)])
                emit("dve", lambda e, b=b, half=half, tt=tt: e.tensor_copy(
                    out=xT[:, half * 4:half * 4 + 4, tt * 128:tt * 128 + np_],
                    in_=bank(b).rearrange("p (q n) -> p q n", q=4)[:, :, :np_]),
                    reads=[bbuf(b)], writes=xbuf[half * 4:half * 4 + 4])

        stage("load")
        for l in range(DEPTH):
            layer(kind, c, l, N, NT, np_, tok0, psrc, nk_d, nv_d, npool_d)

        youts = [((attT if i < 4 else praw)[:, i % 4, :N], buf(("att%d" if i < 4 else "praw%d") % (i % 4))) for i in range(8)]
        rmsnorm(N, [(xT[:, i, :N], xbuf[i]) for i in range(8)], [gcol(DEPTH, 0, i) for i in range(8)], youts, D)
        for tt in range(NT):
            for half in range(2):
                b = zrr.get()
                for q in range(4):
                    i = half * 4 + q
                    emit("pe", lambda e, b=b, q=q, i=i, tt=tt: e.transpose(bank(b)[:np_, q * 128:(q + 1) * 128],
                                                                           youts[i][0][:, tt * 128:tt * 128 + np_], ident32),
                         reads=[youts[i][1], buf("cmat32")], writes=[bbuf(b)])
                emit("dve", lambda e, b=b, half=half: e.tensor_copy(out=xstg[:np_, half * 512:(half + 1) * 512], in_=bank(b)[:np_, :]),
                     reads=[bbuf(b)], writes=[buf("xstg")])
            emit("sp", lambda e, tt=tt: e.dma_start(out=ydst[tok0 + tt * 128: tok0 + tt * 128 + np_, :], in_=xstg[:np_, :]),
                 reads=[buf("xstg")], dma_sem="xstg")

    def layer(kind, c, l, N, NT, np_, tok0, psrc, nk_d, nv_d, npool_d):
        prompt = kind == "p"
        rmsnorm(N, [(xT[:, i, :N], xbuf[i]) for i in range(8)], [gcol(l, 0, i) for i in range(8)],
                [(hT[:, i, :N], hbuf[i]) for i in range(8)], D)
        stage("norm1")
        ri, rb = load_slot(l, 0)
        W = ring[:, ri, :].rearrange("p (k n) -> p k n", k=8)
        for pr in range(4):
            b = zrr.get()
            mm_group(N, b, [W[:, k, pr * 128:(pr + 1) * 128] for k in range(8)], [hT[:, k, :N] for k in range(8)], [rb] + hbuf)
            emit("dve", lambda e, b=b, pr=pr: e.tensor_scalar(out=qT[:, pr, :N], in0=bank(b)[:, :N], scalar1=0.125, scalar2=None, op0=ALU.mult),
                 reads=[bbuf(b)], writes=[buf("qT%d" % pr)])
        stage("q")
        ri, rb = load_slot(l, 1)
        W = ring[:, ri, :].rearrange("p (k n) -> p k n", k=8)
        for pr in range(4):
            b = zrr.get()
            mm_group(N, b, [W[:, k, pr * 128:(pr + 1) * 128] for k in range(8)], [hT[:, k, :N] for k in range(8)], [rb] + hbuf)
            emit("act", lambda e, b=b, pr=pr: e.activation(out=kT[:, pr, :N], in_=bank(b)[:, :N], func=AF.Copy),
                 reads=[bbuf(b)], writes=[buf("kT%d" % pr)])
        if prompt:
            emit("sp", lambda e: e.dma_start(out=kts[l, :, :, tok0:tok0 + N].rearrange("a p s -> p a s"), in_=kT[:, :, :N]),
                 reads=[buf("kT%d" % pr) for pr in range(4)], writes=[buf("kts%d" % l)], dma_sem="ktst")
        for tt in range(NT):
            b = zrr.get()
            j = tt % 2
            mm_group(512, b, [hT[:, k, tt * 128:tt * 128 + np_] for k in range(8)], [W[:, k, :] for k in range(8)], [rb] + hbuf, np_=np_)
            emit("dve", lambda e, b=b, j=j: e.tensor_copy(out=stg[:np_, j, :], in_=bank(b)[:np_, :]),
                 reads=[bbuf(b)], writes=[buf("stg%d" % j)])
            emit("sp", lambda e, j=j, tt=tt: e.dma_start(
                out=nk_d[l, :, tok0 + tt * 128: tok0 + tt * 128 + np_, :].rearrange("h t d -> t h d"),
                in_=stg[:np_, j, :].rearrange("t (h d) -> t h d", h=8)),
                reads=[buf("stg%d" % j)], dma_sem="stg%d" % j)
        stage("k")
        ri, rb = load_slot(l, 2)
        W = ring[:, ri, :].rearrange("p (k n) -> p k n", k=8)
        for tt in range(NT):
            b = zrr.get()
            j = tt % 2
            mm_group(512, b, [hT[:, k, tt * 128:tt * 128 + np_] for k in range(8)], [W[:, k, :] for k in range(8)], [rb] + hbuf, np_=np_)
            emit("dve", lambda e, b=b, j=j: e.tensor_copy(out=stg[:np_, j, :], in_=bank(b)[:np_, :]),
                 reads=[bbuf(b)], writes=[buf("stg%d" % j)])
            emit("act", lambda e, j=j, tt=tt: e.activation(out=vtok[:np_, tt, :, 0:64], in_=stg[:np_, j, :].rearrange("t (h d) -> t h d", h=8), func=AF.Copy),
                 reads=[buf("stg%d" % j)], writes=[buf("vtok%d" % tt)])
            emit("sp", lambda e, j=j, tt=tt: e.dma_start(
                out=nv_d[l, :, tok0 + tt * 128: tok0 + tt * 128 + np_, :].rearrange("h t d -> t h d"),
                in_=stg[:np_, j, :].rearrange("t (h d) -> t h d", h=8)),
                reads=[buf("stg%d" % j)], dma_sem="stg%d" % j)
            if prompt:
                kb = tok0 // 128 + tt
                for a_ in range(4):
                    emit("sp", lambda e, tt=tt, kb=kb, a_=a_: e.dma_start(
                        out=vsc[l, a_, :, kb, :].rearrange("p (h d) -> p h d", h=2),
                        in_=vtok[:, tt, 2 * a_:2 * a_ + 2, 0:64]),
                        reads=[buf("vtok%d" % tt)], writes=[buf("vsc%d" % l)], dma_sem="vst")
        stage("v")
        ri, rb = load_slot(l, 3)
        W = ring[:, ri, :].rearrange("p (k n) -> p k n", k=8)
        if prompt:
            emit("pool", lambda e: e.tensor_copy(out=upx[:, :, 0:16], in_=khalo[:, l, :, :]),
                 reads=[buf("khalo%d" % l)], writes=[buf("upxh")])
        else:
            emit("pool", lambda e: e.memset(upx[:, :, 0:16], 0.0), writes=[buf("upxh")])
            for g in range(4):
                emit("sp", lambda e, g=g: e.dma_start(out=upx[:, g, 1:16], in_=stp[l, :, g * 128:(g + 1) * 128].rearrange("t p -> p t")),
                     writes=[buf("upxh")], dma_sem="upxh")
        for g in range(4):
            b = zrr.get()
            mm_group(N, b, [W[:, k, g * 128:(g + 1) * 128] for k in range(8)], [hT[:, k, :N] for k in range(8)], [rb] + hbuf)
            emit("act", lambda e, b=b, g=g: e.activation(out=upx[:, g, 16:16 + N], in_=bank(b)[:, :N], func=AF.Copy),
                 reads=[bbuf(b)], writes=[buf("upx%d" % g)])
        upall = [buf("upx%d" % g) for g in range(4)]
        if prompt:
            emit("pool", lambda e: e.tensor_copy(out=khalo[:, l, :, 1:16], in_=upx[:, :, 16 + N - 15:16 + N]),
                 reads=upall, writes=[buf("khalo%d" % l)])
        if (not prompt) or c == NCH - 1:
            for g in range(4):
                emit("sp", lambda e, g=g: e.dma_start(out=npool_d[l, :, g * 128:(g + 1) * 128].rearrange("t p -> p t"),
                                                      in_=upx[:, g, 16 + N - 15:16 + N]),
                     reads=[buf("upx%d" % g), buf("upxh")], dma_sem="npool")

        stage("up")
        L = 16 + N
        for g, w in enumerate(WINDOWS):
            src = upx[:, g, :]
            srcb = [buf("upx%d" % g), buf("upxh")]
            sh = 1
            for step in range(g + 1):
                dsti = step % 3
                dst = pt[:, dsti, :]
                lo = 2 * sh - 1
                emit("pool", lambda e, src=src, dst=dst, lo=lo, sh=sh: e.tensor_tensor(
                    out=dst[:, lo:L], in0=src[:, lo:L], in1=src[:, lo - sh:L - sh], op=ALU.add),
                    reads=srcb, writes=[buf("pt%d" % dsti)])
                src = dst
                srcb = [buf("pt%d" % dsti)]
                sh *= 2
            j = g % 2
            emit("dve", lambda e, src=src, g=g, j=j, w=w: e.scalar_tensor_tensor(
                out=dT[:, j, :N], in0=src[:, 16:16 + N], scalar=1.0 / w, in1=upx[:, g, 16:16 + N], op0=ALU.mult, op1=ALU.subtract),
                reads=srcb + [buf("upx%d" % g)], writes=[buf("dT%d" % j)])
            if prompt and c == 0:
                emit("dve", lambda e, src=src, g=g: e.tensor_tensor(out=tmp32[:, 0, 0:16], in0=src[:, 16:32], in1=invc[:, g, :], op=ALU.mult),
                     reads=srcb + [buf("invc")], writes=[buf("tmp0")])
                emit("dve", lambda e, g=g, j=j: e.tensor_tensor(out=dT[:, j, 0:16], in0=tmp32[:, 0, 0:16], in1=upx[:, g, 16:32], op=ALU.subtract),
                     reads=[buf("tmp0"), buf("upx%d" % g)], writes=[buf("dT%d" % j)])
            b = zrr.get()
            emit("pe", lambda e, b=b, g=g, j=j: e.matmul(bank(b)[:, :N], lhsT=wpool[:, 4 * l + g, :], rhs=dT[:, j, :N], start=True, stop=True),
                 reads=[buf("dT%d" % j), buf("wpool")], writes=[bbuf(b)])
            emit("act", lambda e, b=b, g=g: e.activation(out=praw[:, g, :N], in_=bank(b)[:, :N], func=AF.Copy),
                 reads=[bbuf(b)], writes=[buf("praw%d" % g)])
        rmsnorm(N, [(praw[:, g, :N], buf("praw%d" % g)) for g in range(4)], [gcol(l, 4, g) for g in range(4)],
                [(mixT[:, 4 + g, :N], mixbuf[4 + g]) for g in range(4)], PW)

        stage("pool")
        attention(kind, c, l, N, np_)
        rmsnorm(N, [(attT[:, g, :N], buf("att%d" % g)) for g in range(4)], [gcol(l, 3, g) for g in range(4)],
                [(mixT[:, g, :N], mixbuf[g]) for g in range(4)], 512)

        stage("att")
        for s in range(2):
            ri, rb = load_slot(l, 4 + s)
            W = ring[:, ri, :].rearrange("p (k n) -> p k n", k=8)
            for o in range(4):
                ot = s * 4 + o
                b = zrr.get()
                mm_group(N, b, [W[:, k, o * 128:(o + 1) * 128] for k in range(8)], [mixT[:, k, :N] for k in range(8)], [rb] + mixbuf)
                emit("dve", lambda e, b=b, ot=ot: e.tensor_tensor(out=xT[:, ot, :N], in0=xT[:, ot, :N], in1=bank(b)[:, :N], op=ALU.add),
                     reads=[bbuf(b)], writes=[xbuf[ot]])

        stage("outproj")
        emit("sp", lambda e: e.dma_start(out=pstg[:np_, :NT, :], in_=psrc[l, tok0:tok0 + N, :].rearrange("(t p) f -> p t f", p=np_)),
             writes=[buf("pstg")], dma_sem="pstg")
        for k in range(2):
            b = zrr.get()
            for tt in range(NT):
                emit("pe", lambda e, b=b, k=k, tt=tt: e.transpose(bank(b)[:, tt * 128:tt * 128 + np_],
                                                                  pstg[:np_, tt, k * 128:(k + 1) * 128], ident32[:np_, :np_]),
                     reads=[buf("pstg"), buf("cmat32")], writes=[bbuf(b)])
            emit("act", lambda e, b=b, k=k: e.activation(out=pT[:, k, :N], in_=bank(b)[:, :N], func=AF.Copy),
                 reads=[bbuf(b)], writes=[buf("pT%d" % k)])
        rb2 = load_wple(l)
        Wp = wple_t[:, :].rearrange("p (k n) -> p k n", k=2)
        rmsnorm(N, [(xT[:, i, :N], xbuf[i]) for i in range(8)], [gcol(l, 1, i) for i in range(8)],
                [(hT[:, i, :N], hbuf[i]) for i in range(8)], D)
        for i in range(11):
            ri, rb = load_slot(l, 6 + i)
            W = ring[:, ri, :].rearrange("p (k n) -> p k n", k=16)
            for j in range(2):
                ff = 2 * i + j
                bg = zrr.get()
                mm_group(N, bg, [W[:, k, j * 128:(j + 1) * 128] for k in range(8)], [hT[:, k, :N] for k in range(8)], [rb] + hbuf)
                bu = zrr.get()
                mm_group(N, bu, [W[:, 8 + k, j * 128:(j + 1) * 128] for k in range(8)], [hT[:, k, :N] for k in range(8)], [rb] + hbuf)
                jj = ff % 2
                emit("act", lambda e, bg=bg, jj=jj: e.activation(out=sgt[:, jj, :N], in_=bank(bg)[:, :N], func=AF.Silu),
                     reads=[bbuf(bg)], writes=[buf("sgt%d" % jj)])
                emit("dve", lambda e, bu=bu, jj=jj, ff=ff: e.tensor_tensor(out=actT[:, ff, :N], in0=sgt[:, jj, :N], in1=bank(bu)[:, :N], op=ALU.mult),
                     reads=[bbuf(bu), buf("sgt%d" % jj)], writes=[buf("act%d" % ff)])
        actbufs = [buf("act%d" % ff) for ff in range(22)]
        for ot in range(8):
            ri, rb = load_slot(l, 17 + ot, nelem=2816)
            W = ring[:, ri, 0:2816].rearrange("p (k n) -> p k n", k=22)
            b = zrr.get()
            mm_group(N, b, [W[:, k, :] for k in range(22)], [actT[:, k, :N] for k in range(22)], [rb] + actbufs)
            emit("dve", lambda e, b=b, ot=ot: e.tensor_tensor(out=xT[:, ot, :N], in0=xT[:, ot, :N], in1=bank(b)[:, :N], op=ALU.add),
                 reads=[bbuf(b)], writes=[xbuf[ot]])

        stage("ffn")
        rmsnorm(N, [(xT[:, i, :N], xbuf[i]) for i in range(8)], [gcol(l, 2, i) for i in range(8)],
                [(hT[:, i, :N], hbuf[i]) for i in range(8)], D)
        for s in range(2):
            ri, rb = load_slot(l, 25 + s)
            W = ring[:, ri, :].rearrange("p (k n) -> p k n", k=8)
            for o in range(4):
                ot = s * 4 + o
                bg = zrr.get()
                mm_group(N, bg, [W[:, k, o * 128:(o + 1) * 128] for k in range(8)], [hT[:, k, :N] for k in range(8)], [rb] + hbuf)
                bp = zrr.get()
                mm_group(N, bp, [Wp[:, k, ot * 128:(ot + 1) * 128] for k in range(2)], [pT[:, k, :N] for k in range(2)],
                         [rb2, buf("pT0"), buf("pT1")])
                jj = ot % 2
                emit("act", lambda e, bg=bg, jj=jj: e.activation(out=sgt[:, jj, :N], in_=bank(bg)[:, :N], func=AF.Sigmoid),
                     reads=[bbuf(bg)], writes=[buf("sgt%d" % jj)])
                emit("dve", lambda e, bp=bp, jj=jj: e.tensor_tensor(out=tmp32[:, jj, :N], in0=sgt[:, jj, :N], in1=bank(bp)[:, :N], op=ALU.mult),
                     reads=[bbuf(bp), buf("sgt%d" % jj)], writes=[buf("tmp%d" % jj)])
                emit("pool", lambda e, jj=jj, ot=ot: e.tensor_tensor(out=xT[:, ot, :N], in0=xT[:, ot, :N], in1=tmp32[:, jj, :N], op=ALU.add),
                     reads=[buf("tmp%d" % jj)], writes=[xbuf[ot]])

    att_state = {"i": 0, "seg": 0}

    def attention(kind, c, l, N, np_):
        for pr in range(4):
            attention_pair(kind, c, l, N, np_, pr)

    def attention_pair(kind, c, l, N, np_, pr):
        prompt = kind == "p"
        if True:
            groups = []
            if prompt:
                diag = []
                for j in (3, 2, 1, 0):
                    diag.append((lambda hh, j=j: kT[64 * hh:64 * hh + 64, pr, j * 128:(j + 1) * 128],
                                 lambda hh, j=j: vtok[:, j, 2 * pr + hh, :], 128, masks[:, j, :N],
                                 [buf("kT%d" % pr), buf("vtok%d" % j)], j))
                groups.append((None, diag))
                nold = 4 * c
                nseg = (nold + segb - 1) // segb
                for sg_ in range(nseg - 1, -1, -1):
                    b0 = sg_ * segb
                    nb = min(segb, nold - b0)
                    groups.append(((b0, nb), None))
            else:
                diag = [(lambda hh: kT[64 * hh:64 * hh + 64, pr, 0:DS], lambda hh: vtok[:DS, 0, 2 * pr + hh, :], DS,
                         masks[:DS, 0, :DS], [buf("kT%d" % pr), buf("vtok0")], 0)]
                groups.append((None, diag))
                groups.append((("cache", 8), None))

            tasks = []
            first = [True, True]
            seg_first = []
            seg_loaders = []
            for (segdesc, blocks) in groups:
                if blocks is None:
                    si = att_state["seg"] % 2
                    att_state["seg"] += 1
                    kb_ = buf("ktseg%d" % si)
                    vb_ = buf("vseg%d" % si)
                    if segdesc[0] == "cache":
                        nb = 8

                        def loader(si=si, kb_=kb_, vb_=vb_):
                            for kb in range(8):
                                emit("sp", lambda e, kb=kb: e.dma_start(
                                    out=pstg[:, 0, 0:128].rearrange("p (h d) -> p h d", h=2),
                                    in_=ck[l, 2 * pr:2 * pr + 2, kb * 128:(kb + 1) * 128, :].rearrange("h k d -> k h d")),
                                    writes=[buf("pstg")], dma_sem="pstg")
                                q4 = kb % 4
                                if q4 == 0:
                                    bq = zrr.get()
                                emit("pe", lambda e, bq=bq, q4=q4: e.transpose(bank(bq)[:, q4 * 128:(q4 + 1) * 128], pstg[:, 0, 0:128], ident32),
                                     reads=[buf("pstg"), buf("cmat32")], writes=[bbuf(bq)])
                                if q4 == 3:
                                    emit("dve", lambda e, bq=bq, kb=kb, si=si: e.tensor_copy(out=ktseg[:, si, (kb - 3) * 128:(kb + 1) * 128], in_=bank(bq)[:, :]),
                                         reads=[bbuf(bq)], writes=[kb_])
                            for half in range(2):
                                for h2 in range(2):
                                    emit("sp", lambda e, half=half, h2=h2: e.dma_start(
                                        out=xstg[:, 0:512].rearrange("p (k h d) -> p k h d", k=4, h=2)[:, :, h2, :],
                                        in_=cv[l, 2 * pr + h2, half * 512:(half + 1) * 512, :].rearrange("(k p) d -> p k d", p=128)),
                                        writes=[buf("xstg")], dma_sem="xstg")
                                for h2 in range(2):
                                    emit("dve", lambda e, half=half, si=si, h2=h2: e.tensor_copy(
                                        out=vseg[:, si, half * 4:(half + 1) * 4, h2, 0:64],
                                        in_=xstg[:, 0:512].rearrange("p (k h d) -> p k h d", k=4, h=2)[:, :, h2, :]),
                                        reads=[buf("xstg")], writes=[vb_])
                    else:
                        b0, nb = segdesc

                        def loader(si=si, b0=b0, nb=nb, kb_=kb_, vb_=vb_):
                            emit("sp", lambda e: e.dma_start(out=ktseg[:, si, 0:nb * 128], in_=kts[l, pr, :, b0 * 128:(b0 + nb) * 128]),
                                 reads=[buf("kts%d" % l)], writes=[kb_], dma_sem="ktseg%d" % si)
                            for h2 in range(2):
                                emit("sp", lambda e, h2=h2: e.dma_start(out=vseg[:, si, 0:nb, h2, 0:64], in_=vsc[l, pr, :, b0:b0 + nb, h2 * 64:(h2 + 1) * 64]),
                                     reads=[buf("vsc%d" % l)], writes=[vb_], dma_sem="vseg%d" % si)
                    seg_first.append(len(tasks))
                    ns_ = len(seg_first)
                    trig = 0 if ns_ <= 2 else seg_first[ns_ - 2] + 4
                    seg_loaders.append((trig, loader))
                    blocks = []
                    for kk in range(nb - 1, -1, -1):
                        blocks.append((lambda hh, kk=kk, si=si: ktseg[64 * hh:64 * hh + 64, si, kk * 128:(kk + 1) * 128],
                                       lambda hh, kk=kk, si=si: vseg[:, si, kk, hh, :], 128, None, [kb_, vb_], 0))
                for blk in blocks:
                    for hh in range(2):
                        tasks.append((blk, hh))

            ntask = len(tasks)
            last_idx = [max(i for i in range(ntask) if tasks[i][1] == hh) for hh in range(2)]
            st = [dict() for _ in range(ntask)]
            ob = [0, 1]

            def stage1(i):
                (ktf, vap, nk, mk, rds, q0), hh = tasks[i]
                c0 = 128 * q0
                zb = zrr.get()
                ii = att_state["i"]
                att_state["i"] += 1
                st[i].update(zb=zb, e=ii % 2, s=ii % 3, a=ii % 3)
                emit("pe", lambda e: e.matmul(bank(zb)[:nk, c0:N], lhsT=ktf(hh), rhs=qT[64 * hh:64 * hh + 64, pr, c0:N], start=True, stop=False),
                     reads=[rds[0], buf("qT%d" % pr)], writes=[bbuf(zb)])
                ei, si_ = st[i]["e"], st[i]["s"]
                emit("act", lambda e: e.activation(out=e32[:nk, ei, c0:N], in_=bank(zb)[:nk, c0:N], func=AF.Exp),
                     reads=[bbuf(zb)], writes=[buf("e32_%d" % ei)])
                st[i]["nk"] = nk

            def stage1b(i):
                (ktf, vap, nk, mk, rds, q0), hh = tasks[i]
                c0 = 128 * q0
                ei, si_ = st[i]["e"], st[i]["s"]
                emit("act", lambda e: e.activation(out=spb[:nk, si_, c0:N], in_=e32[:nk, ei, c0:N], func=AF.Ln, bias=onec[:nk, 0:1], scale=1.0),
                     reads=[buf("e32_%d" % ei), buf("epsc")], writes=[buf("spb%d" % si_)])
                if mk is not None:
                    emit("pool", lambda e: e.tensor_tensor(out=spb[:nk, si_, c0:N], in0=spb[:nk, si_, c0:N], in1=mk[:nk, c0:N], op=ALU.mult),
                         reads=[buf("masks")], writes=[buf("spb%d" % si_)])

            NQT = (N + 127) // 128
            npq = min(N, 128)
            accb = [buf("accA"), buf("accB")]
            emit("pool", lambda e: e.memset(acc[:], 0.0), writes=accb)
            emit("dve", lambda e: e.memset(Rst[:], 1.0), writes=[buf("Rst0"), buf("Rst1")])

            def stage2(i):
                (ktf, vapf, nk, mk, rds, q0), hh = tasks[i]
                c0 = 128 * q0
                zb, si_ = st[i]["zb"], st[i]["s"]
                emit("pe", lambda e: e.matmul(bank(zb)[:nk, c0:N], lhsT=negtri[:nk, :nk], rhs=spb[:nk, si_, c0:N], start=False, stop=True),
                     reads=[buf("spb%d" % si_), buf("cmatb")], writes=[bbuf(zb)])

            def stage3(i):
                (ktf, vapf, nk, mk, rds, q0), hh = tasks[i]
                c0 = 128 * q0
                zb, ai = st[i]["zb"], st[i]["a"]
                emit("act", lambda e: e.activation(out=ab[:nk, ai, c0:N], in_=bank(zb)[:nk, c0:N], func=AF.Exp),
                     reads=[bbuf(zb)], writes=[buf("ab%d" % ai)])
                if mk is not None:
                    emit("dve", lambda e: e.tensor_tensor(out=ab[:nk, ai, c0:N], in0=ab[:nk, ai, c0:N], in1=mk[:nk, c0:N], op=ALU.mult),
                         reads=[buf("masks")], writes=[buf("ab%d" % ai)])

            def stage4(i):
                (ktf, vapf, nk, mk, rds, q0), hh = tasks[i]
                c0 = 128 * q0
                ai = st[i]["a"]
                pb = i % 2
                pj = i % 2
                for qt in range(q0, NQT):
                    emit("pe", lambda e, qt=qt: e.matmul(bank(pb)[:npq, qt * 65:(qt + 1) * 65], lhsT=ab[:nk, ai, qt * 128:qt * 128 + npq],
                                                       rhs=vapf(hh), start=True, stop=True),
                         reads=[buf("ab%d" % ai), rds[1]], writes=[bbuf(pb)])
                P = bank(pb)[:npq, 0:NQT * 65].rearrange("p (q c) -> p q c", c=65)
                emit("dve", lambda e: e.tensor_tensor(out=tmpP[:npq, pj, q0:NQT, :], in0=P[:, q0:NQT, 0:64],
                                                      in1=Rst[:npq, hh, q0:NQT].unsqueeze(2).to_broadcast([npq, NQT - q0, 64]), op=ALU.mult),
                     reads=[bbuf(pb), buf("Rst%d" % hh)], writes=[buf("tmpP%d" % pj)])
                emit("pool", lambda e: e.tensor_tensor(out=acc[:npq, q0:NQT, 64 * hh:64 * hh + 64], in0=acc[:npq, q0:NQT, 64 * hh:64 * hh + 64],
                                                       in1=tmpP[:npq, pj, q0:NQT, :], op=ALU.add),
                     reads=[buf("tmpP%d" % pj)], writes=[accb[hh]])
                if i != last_idx[hh]:
                    emit("dve", lambda e: e.tensor_scalar(out=mmt[:npq, hh, q0:NQT], in0=P[:, q0:NQT, 64], scalar1=-1.0, scalar2=1.0, op0=ALU.mult, op1=ALU.add),
                         reads=[bbuf(pb)], writes=[buf("mmt%d" % hh)])
                    emit("dve", lambda e: e.scalar_tensor_tensor(out=Rst[:npq, hh, q0:NQT], in0=mmt[:npq, hh, q0:NQT], scalar=0.0, in1=Rst[:npq, hh, q0:NQT],
                                                                 op0=ALU.max, op1=ALU.mult),
                         reads=[buf("mmt%d" % hh)], writes=[buf("Rst%d" % hh)])
                if i == max(last_idx):
                    bq_ = zrr.get()
                    for qt in range(NQT):
                        emit("pe", lambda e, qt=qt: e.transpose(bank(bq_)[:, qt * 128:qt * 128 + npq], acc[:npq, qt, :], ident32[:npq, :npq]),
                             reads=accb + [buf("cmat32")], writes=[bbuf(bq_)])
                    emit("act", lambda e: e.activation(out=attT[:, pr, :N], in_=bank(bq_)[:, :N], func=AF.Copy),
                         reads=[bbuf(bq_)], writes=[buf("att%d" % pr)])

            for step in range(ntask + 3):
                for (trig, fn_) in seg_loaders:
                    if trig == step:
                        fn_()
                if step < ntask:
                    stage1(step)
                if 0 <= step - 2 < ntask:
                    stage3(step - 2)
                if step < ntask:
                    stage1b(step)
                if 0 <= step - 1 < ntask:
                    stage2(step - 1)
                if 0 <= step - 3 < ntask:
                    stage4(step - 3)

    onec = sb("onec", [128, 1], F32)

    emit("dve", lambda e: e.memset(epsc[:], EPS), writes=[buf("epsc")])
    emit("dve", lambda e: e.memset(onec[:], 1.0), writes=[buf("epsc")])
    prologue()
    stage("prologue")
    try:
        if DO_SAMPLE:
            process("s", 0)
        for c in range(NCH):
            process("p", c)
    except _Stop:
        pass

    final_waits = [((k, 0), v) for k, v in sc.dsem.items()]

    semv = {}
    ptr = {e: 0 for e in Sched.ENGS}
    progress = True
    while progress:
        progress = False
        for e_ in Sched.ENGS:
            ol = sc.ops[e_]
            while ptr[e_] < len(ol):
                waits, fn, inc = ol[ptr[e_]]
                if all(semv.get(k, 0) >= v for (k, v) in waits):
                    semv[inc[0]] = semv.get(inc[0], 0) + inc[1]
                    ptr[e_] += 1
                    progress = True
                else:
                    break
    for e_ in Sched.ENGS:
        if ptr[e_] < len(sc.ops[e_]):
            waits, fn, inc = sc.ops[e_][ptr[e_]]
            raise RuntimeError("schedule deadlock: %s stuck at op %d waits %s have %s" % (
                e_, ptr[e_], waits, [(k, semv.get(k, 0)) for k, v in waits]))

    sem_names = set()
    for eng in Sched.ENGS:
        for (waits, fn, inc) in sc.ops[eng]:
            sem_names.add(inc[0])
            for (k, v) in waits:
                sem_names.add(k)
    for k, v in final_waits:
        sem_names.add(k)
    sem_names = sorted(sem_names, key=str)
    from contextlib import ExitStack
    with ExitStack() as es:
        sems = {}
        for k in sem_names:
            sems[k] = es.enter_context(nc.semaphore("s_%s_%d" % (k[0], k[1])))
        es.enter_context(nc.allow_non_contiguous_dma(reason="small transposing state DMAs"))
        block = es.enter_context(nc.Block())

        def runner(engname):
            def f(e):
                for (waits, fn, inc) in sc.ops[engname]:
                    for (k, v) in waits:
                        e.wait_ge(sems[k], v)
                    ins = fn(e)
                    ins.then_inc(sems[inc[0]], inc[1])
                if engname == "sp":
                    for (k, v) in final_waits:
                        e.wait_ge(sems[k], v)
            return f

        block.tensor(runner("pe"))
        block.scalar(runner("act"))
        block.vector(runner("dve"))
        block.gpsimd(runner("pool"))
        block.sync(runner("sp"))
    return nc, {e: len(sc.ops[e]) for e in Sched.ENGS}


def _consts():
    p = np.arange(128)[:, None]
    t = np.arange(512)[None, :]
    masks = np.zeros((128, 4, 512), np.float32)
    for j in range(4):
        masks[:, j, :] = ((128 * j + p) < t).astype(np.float32)
    invc = np.zeros((128, 4, 16), np.float32)
    for g, w in enumerate(WINDOWS):
        invc[:, g, :] = 1.0 / np.minimum(w, np.arange(16) + 1).astype(np.float32)[None, :]
    cm = np.zeros((128, 4, 128), np.float32)
    cm[:, 0, :] = np.eye(128, dtype=np.float32)
    j = np.arange(128)[:, None]
    s_ = np.arange(128)[None, :]
    cm[:, 1, :] = -(j >= s_).astype(np.float32)
    cm[:, 2, :] = -1.0
    cm[:, 3, :] = 1.0
    return masks, invc, cm


def _gcols(g_mix, g_ffn, g_ple, g_sb_out, pool_scale, g_final):
    out = np.zeros((128, 136), np.float32)
    for l in range(DEPTH):
        out[:, 32 * l + 0:32 * l + 8] = g_mix[l].reshape(8, 128).T
        out[:, 32 * l + 8:32 * l + 16] = g_ffn[l].reshape(8, 128).T
        out[:, 32 * l + 16:32 * l + 24] = g_ple[l].reshape(8, 128).T
        out[:, 32 * l + 24:32 * l + 28] = g_sb_out[l].reshape(4, 128).T
        out[:, 32 * l + 28:32 * l + 32] = pool_scale[l].reshape(4, 128).T
    out[:, 128:136] = g_final.reshape(8, 128).T
    return out


_CACHE = {}


def kernel(x_prompt, x_sample, cache_k, cache_v, state_pool, p_prompt, p_sample,
           g_mix, w_in, g_sb_out, w_pool, pool_scale, w_out,
           g_ffn, w_ffn_gate, w_ffn_up, w_ffn_down, g_ple, w_ple, w_ple_gate, g_final,
           _nch=16, _sample=True, _cores=8):
    f = lambda a: np.ascontiguousarray(np.asarray(a, dtype=np.float32))
    key = (_nch, _sample)
    if key not in _CACHE:
        _CACHE[key] = build(_nch, _sample)
    nc, counts = _CACHE[key]
    masks, invc, cm = _consts()
    gc = _gcols(f(g_mix), f(g_ffn), f(g_ple), f(g_sb_out), f(pool_scale), f(g_final))
    shared = dict(w_in=f(w_in), w_pool=f(w_pool), w_out=f(w_out), w_g=f(w_ffn_gate), w_u=f(w_ffn_up), w_d=f(w_ffn_down),
                  w_ple=f(w_ple), w_pg=f(w_ple_gate), gcols=gc, masks=masks, invc=invc, cmat=cm)
    x_prompt = f(x_prompt); p_prompt = f(p_prompt); x_sample = f(x_sample); p_sample = f(p_sample)
    cache_k = f(cache_k); cache_v = f(cache_v); state_pool = f(state_pool)
    in_maps = []
    for c in range(8):
        b = c % NBATCH
        m = dict(shared)
        m.update(xp=x_prompt[b], pp=np.ascontiguousarray(p_prompt[:, b]), xs=x_sample[c],
                 ps=np.ascontiguousarray(p_sample[:, c]), ck=np.ascontiguousarray(cache_k[:, c]),
                 cv=np.ascontiguousarray(cache_v[:, c]), stp=np.ascontiguousarray(state_pool[:, c]))
        in_maps.append(m)
    res = run_bass_kernel_spmd(nc, in_maps[:_cores], core_ids=list(range(_cores)))
    r = list(res.results) + [res.results[0]] * (8 - _cores)
    y_prompt = np.stack([r[b]["yp"] for b in range(NBATCH)])
    y_sample = np.stack([r[c]["ys"] for c in range(8)])
    nk_p = np.stack([r[b]["nkp"] for b in range(NBATCH)], axis=1)
    nv_p = np.stack([r[b]["nvp"] for b in range(NBATCH)], axis=1)
    np_p = np.stack([r[b]["npp"] for b in range(NBATCH)], axis=1)
    nk_s = np.stack([r[c]["nks"] for c in range(8)], axis=1)
    nv_s = np.stack([r[c]["nvs"] for c in range(8)], axis=1)
    np_s = np.stack([r[c]["nps"] for c in range(8)], axis=1)
    return (y_prompt.astype(np.float32), y_sample.astype(np.float32), nk_p.astype(np.float32), nv_p.astype(np.float32),
            np_p.astype(np.float32), nk_s.astype(np.float32), nv_s.astype(np.float32), np_s.astype(np.float32))
```
